# Optimizing a Trainium2 kernel written in Bass

```python
import jax, jax.numpy as jnp
from jax import lax
import numpy as np

D_MODEL = 1024
BATCH = 16
SEQ = 2048
DEPTH = 1

N_MEM = 256
GRID_W = 64
HEAD_DIM = 64
N_Q_HEADS = D_MODEL // 128
N_KV_HEADS = N_Q_HEADS // 4
Q_PER_KV = N_Q_HEADS // N_KV_HEADS
ATTN_WIDTH = N_Q_HEADS * HEAD_DIM
KV_WIDTH = N_KV_HEADS * HEAD_DIM
POOL_WINDOWS = (2, 4, 8, 16)
N_POOL_GROUPS = len(POOL_WINDOWS)
POOL_WIDTH = D_MODEL // 2
POOL_GROUP = POOL_WIDTH // N_POOL_GROUPS
N_BRANCHES = 2
IN_WIDTH = ATTN_WIDTH + 2 * KV_WIDTH + POOL_WIDTH + N_BRANCHES * D_MODEL
Q_BLOCK = 128
ROPE_THETA = 10000.0
ROPE_AXIS_DIM = HEAD_DIM // 2
ROPE_FREQS = ROPE_AXIS_DIM // 2
N_X_HEADS = 4
X_HEAD_DIM = D_MODEL // N_X_HEADS
D_FF = 4 * D_MODEL
EPS = 1e-6

kernel_name = "hybrid_gqa_pool_memory_encoder"


def rmsnorm(x, g):
    xf = x.astype(jnp.float32)
    y = xf * lax.rsqrt(jnp.mean(xf * xf, axis=-1, keepdims=True) + EPS)
    return (y * g.astype(jnp.float32)).astype(x.dtype)


def axial_rope_tables(seq_len):
    rows = seq_len // GRID_W
    row = jnp.repeat(jnp.arange(rows), GRID_W)
    col = jnp.tile(jnp.arange(GRID_W), rows)
    inv = ROPE_THETA ** (-jnp.arange(0, ROPE_AXIS_DIM, 2, dtype=jnp.float32) / ROPE_AXIS_DIM)
    ang = jnp.stack([row, col], axis=1).astype(jnp.float32)[:, :, None] * inv
    return jnp.cos(ang)[:, None, :, None, :], jnp.sin(ang)[:, None, :, None, :]


def apply_rope(x, cos, sin):
    b, s, h, _ = x.shape
    xr = x.reshape(b, s, h, 2, 2, ROPE_FREQS).astype(jnp.float32)
    rot = jnp.stack([-xr[..., 1, :], xr[..., 0, :]], axis=-2)
    return (xr * cos + rot * sin).reshape(x.shape).astype(x.dtype)


def gqa_block_attention(q, k, v):
    b, s, kvh, g, hd = q.shape
    nb = s // Q_BLOCK
    qb = q.reshape(b, nb, Q_BLOCK, kvh, g, hd).transpose(1, 0, 2, 3, 4, 5)
    scale = HEAD_DIM ** -0.5

    def one_block(q_blk):
        sc = jnp.einsum('bqkgd,bskd->bkgqs', q_blk, k).astype(jnp.float32) * scale
        p = jax.nn.softmax(sc, axis=-1).astype(v.dtype)
        return jnp.einsum('bkgqs,bskd->bqkgd', p, v)

    o = lax.map(one_block, qb)
    return o.transpose(1, 0, 2, 3, 4, 5).reshape(b, s, kvh * g * hd)


def multiscale_pool(u):
    b, s, ng, c = u.shape
    uf = u.astype(jnp.float32)
    csum = jnp.concatenate([jnp.zeros((b, 1, ng, c), jnp.float32), jnp.cumsum(uf, axis=1)], axis=1)
    t = jnp.arange(s)
    outs = []
    for gi, w in enumerate(POOL_WINDOWS):
        lo = jnp.clip(t - w // 2, 0, s)
        hi = jnp.clip(t + (w - w // 2), 0, s)
        win_sum = csum[:, hi, gi] - csum[:, lo, gi]
        cnt = (hi - lo).astype(jnp.float32)[None, :, None]
        outs.append(win_sum / cnt - uf[:, :, gi])
    return jnp.stack(outs, axis=2).astype(u.dtype)


def setup_inputs(seed: int = 0) -> dict:
    key = jax.random.key(seed)
    ks = jax.random.split(key, 24)
    f32 = jnp.float32

    def w(k, shape, fan_in):
        return jax.random.normal(k, shape, f32) * (fan_in ** -0.5)

    def gain(k, shape):
        return 1.0 + 0.02 * jax.random.normal(k, shape, f32)

    L = DEPTH
    return {
        "x": jax.random.normal(ks[0], (BATCH, SEQ, D_MODEL), f32),
        "mem": jax.random.normal(ks[1], (BATCH, N_MEM, D_MODEL), f32),
        "g_mix": gain(ks[2], (L, D_MODEL)),
        "w_in": w(ks[3], (L, D_MODEL, IN_WIDTH), D_MODEL),
        "b_gate": 0.01 * jax.random.normal(ks[4], (L, N_BRANCHES * D_MODEL), f32),
        "g_q": gain(ks[5], (L, HEAD_DIM)),
        "g_k": gain(ks[6], (L, HEAD_DIM)),
        "w_attn_up": w(ks[7], (L, ATTN_WIDTH, D_MODEL), ATTN_WIDTH),
        "pool_w": w(ks[8], (L, N_POOL_GROUPS, POOL_GROUP, POOL_GROUP), POOL_GROUP),
        "pool_scale": gain(ks[9], (L, POOL_WIDTH)),
        "w_pool_up": w(ks[10], (L, POOL_WIDTH, D_MODEL), POOL_WIDTH),
        "w_out": w(ks[11], (L, D_MODEL, D_MODEL), D_MODEL),
        "g_cross": gain(ks[12], (L, D_MODEL)),
        "g_mem": gain(ks[13], (L, D_MODEL)),
        "w_xq": w(ks[14], (L, D_MODEL, D_MODEL), D_MODEL),
        "w_xkv": w(ks[15], (L, D_MODEL, 2 * D_MODEL), D_MODEL),
        "w_xo": w(ks[16], (L, D_MODEL, D_MODEL), D_MODEL),
        "g_ffn": gain(ks[17], (L, D_MODEL)),
        "w_ff1": w(ks[18], (L, D_MODEL, D_FF), D_MODEL),
        "w_ff2": w(ks[19], (L, D_FF, D_MODEL), D_FF),
        "g_final": gain(ks[20], (D_MODEL,)),
    }


def reference(x, mem, g_mix, w_in, b_gate, g_q, g_k, w_attn_up, pool_w, pool_scale, w_pool_up,
              w_out, g_cross, g_mem, w_xq, w_xkv, w_xo, g_ffn, w_ff1, w_ff2, g_final):
    b, s, d = x.shape
    m_len = mem.shape[1]
    cos, sin = axial_rope_tables(s)
    h = x
    for l in range(DEPTH):
        n1 = rmsnorm(h, g_mix[l])
        proj = n1 @ w_in[l]
        q, k, v, u, gates = jnp.split(
            proj,
            np.cumsum([ATTN_WIDTH, KV_WIDTH, KV_WIDTH, POOL_WIDTH]).tolist(),
            axis=-1)
        q = rmsnorm(q.reshape(b, s, N_Q_HEADS, HEAD_DIM), g_q[l])
        k = rmsnorm(k.reshape(b, s, N_KV_HEADS, HEAD_DIM), g_k[l])
        q = apply_rope(q, cos, sin).reshape(b, s, N_KV_HEADS, Q_PER_KV, HEAD_DIM)
        k = apply_rope(k, cos, sin)
        v = v.reshape(b, s, N_KV_HEADS, HEAD_DIM)
        a = gqa_block_attention(q, k, v) @ w_attn_up[l]
        pooled = multiscale_pool(u.reshape(b, s, N_POOL_GROUPS, POOL_GROUP))
        pm = jnp.einsum('bsgc,gcd->bsgd', pooled, pool_w[l]).reshape(b, s, POOL_WIDTH)
        p = (pm * pool_scale[l]) @ w_pool_up[l]
        gt = jax.nn.sigmoid(gates + b_gate[l])
        g_a, g_p = jnp.split(gt, 2, axis=-1)
        h = h + (g_a * a + g_p * p) @ w_out[l]

        n2 = rmsnorm(h, g_cross[l])
        mn = rmsnorm(mem, g_mem[l])
        xq = (n2 @ w_xq[l]).reshape(b, s, N_X_HEADS, X_HEAD_DIM)
        xk, xv = jnp.split((mn @ w_xkv[l]).reshape(b, m_len, 2, N_X_HEADS, X_HEAD_DIM), 2, axis=2)
        xk, xv = xk[:, :, 0], xv[:, :, 0]
        sc = jnp.einsum('bshd,bmhd->bhsm', xq, xk).astype(jnp.float32) * (X_HEAD_DIM ** -0.5)
        pr = jax.nn.softmax(sc, axis=-1).astype(xv.dtype)
        xo = jnp.einsum('bhsm,bmhd->bshd', pr, xv).reshape(b, s, d)
        h = h + xo @ w_xo[l]

        n3 = rmsnorm(h, g_ffn[l])
        h = h + jnp.square(jax.nn.relu(n3 @ w_ff1[l])) @ w_ff2[l]
    return rmsnorm(h, g_final)
```

```python
import numpy as np
from contextlib import ExitStack
import concourse.bass as bass
import concourse.mybir as mybir
from concourse.bass_utils import run_bass_kernel_spmd

F32 = mybir.dt.float32
BF16 = mybir.dt.bfloat16
ALU = mybir.AluOpType
AF = mybir.ActivationFunctionType

ENGS = ("pe", "act", "dve", "pool", "sp")
EPS = 1e-6
NPIECE = 20
PW = 8192


class Sched:
    SAME_ENG_DIST = 10 ** 9
    SEM_CAP = 20000

    def __init__(self):
        self.ops = []

    def op(self, eng, fn, reads=(), writes=()):
        isps = lambda k: isinstance(k, tuple) and k[0] == "ps"
        w = list(writes) + [k for k in reads if isps(k)]
        r = [k for k in reads if not isps(k)]
        self.ops.append(dict(eng=eng, fn=fn, r=tuple(r), w=tuple(w), dma=None))

    def dma(self, q, fn, semkey, reads=(), writes=(), after=()):
        self.ops.append(dict(eng=q, fn=fn, r=tuple(reads), w=tuple(writes), dma=semkey, after=tuple(after)))

    def analyze(self):
        last_w, readers = {}, {}
        pos = {e: 0 for e in ENGS}
        dma_cnt = {}
        for i, o in enumerate(self.ops):
            deps = set()
            for r in o["r"] + o.get("after", ()):
                if r in last_w:
                    deps.add(last_w[r])
            for w in o["w"]:
                if w in last_w:
                    deps.add(last_w[w])
                deps.update(readers.get(w, ()))
            deps.discard(i)
            o["pos"] = pos[o["eng"]]
            pos[o["eng"]] += 1
            dma_waits, eng_deps = {}, {}
            for d in deps:
                od = self.ops[d]
                if od["dma"] is not None:
                    k = od["dma"]
                    dma_waits[k] = dma_cnt[k] * 16
                else:
                    e2 = od["eng"]
                    if e2 == o["eng"]:
                        if e2 == "pe":
                            continue
                        if o["pos"] - od["pos"] > self.SAME_ENG_DIST:
                            continue
                    if e2 not in eng_deps or self.ops[eng_deps[e2]]["pos"] < od["pos"]:
                        eng_deps[e2] = d
            o["dma_waits"] = dma_waits
            o["eng_deps"] = eng_deps
            for d in eng_deps.values():
                self.ops[d]["ms"] = True
            if o["dma"] is not None:
                dma_cnt[o["dma"]] = dma_cnt.get(o["dma"], 0) + 1
            for r in o["r"]:
                readers.setdefault(r, []).append(i)
            for w in o["w"]:
                last_w[w] = i
                readers[w] = []
        cnt = {e: 0 for e in ENGS}
        for o in self.ops:
            if o.get("ms"):
                o["msidx"] = cnt[o["eng"]]
                cnt[o["eng"]] += 1
        self.ms_count = cnt
        self.dma_keys = list(dma_cnt.keys())
        self.dma_total = dict(dma_cnt)

    def emit(self, nc, stack):
        self.analyze()
        CAP = self.SEM_CAP
        esems = {}
        for e in ENGS:
            n = max(1, (self.ms_count[e] + CAP - 1) // CAP)
            esems[e] = [stack.enter_context(nc.semaphore(f"s_{e}{k}")) for k in range(n)]
        dsems = {k: stack.enter_context(nc.semaphore(f"d_{k}")) for k in self.dma_keys}
        block = stack.enter_context(nc.Block())
        ops = self.ops

        def run(ename, eng):
            waited = {}
            for o in ops:
                if o["eng"] != ename:
                    continue
                for k, v in o["dma_waits"].items():
                    if waited.get(("d", k), 0) < v:
                        eng.wait_ge(dsems[k], v)
                        waited[("d", k)] = v
                for e2, d in o["eng_deps"].items():
                    mi = ops[d]["msidx"]
                    si, v = mi // CAP, mi % CAP + 1
                    if waited.get((e2, si), 0) < v:
                        eng.wait_ge(esems[e2][si], v)
                        waited[(e2, si)] = v
                ins = o["fn"](eng)
                if o["dma"] is not None:
                    ins.then_inc(dsems[o["dma"]], 16)
                elif o.get("ms"):
                    ins.then_inc(esems[ename][o["msidx"] // CAP], 1)

        @block.tensor
        def _(eng):
            run("pe", eng)

        @block.scalar
        def _(eng):
            run("act", eng)

        @block.vector
        def _(eng):
            run("dve", eng)

        @block.gpsimd
        def _(eng):
            run("pool", eng)

        @block.sync
        def _(eng):
            run("sp", eng)
            for k in self.dma_keys:
                eng.wait_ge(dsems[k], self.dma_total[k] * 16)


C_GMIX, C_GCROSS, C_GFFN, C_GFINAL, C_GMEM, C_BG, C_PS, C_GQ, C_GQS, C_GK, C_GKS = 0, 8, 16, 24, 32, 40, 56, 60, 61, 62, 63
PI_W1, PI_W1B, PI_XKV0, PI_XKV1, PI_BLK0 = 0, 1, 2, 3, 4


CK_LOG = []


class _Stop(Exception):
    pass


def build_core(n_seq=2, debug=None, stop_at=None):
    nc = bass.Bass("TRN2", target_bir_lowering=False)
    xT = nc.dram_tensor("xT", [2, 1024, 2048], F32, kind="ExternalInput").ap()
    memT = nc.dram_tensor("memT", [2, 1024, 256], F32, kind="ExternalInput").ap()
    wst = nc.dram_tensor("wst", [NPIECE, 128, PW], F32, kind="ExternalInput").ap()
    vec_d = nc.dram_tensor("vec", [128, 64], F32, kind="ExternalInput").ap()
    rope_d = nc.dram_tensor("rope", [2, 128, 2048], F32, kind="ExternalInput").ap()
    band_d = nc.dram_tensor("band", [128, 12 * 144], F32, kind="ExternalInput").ap()
    poolw_d = nc.dram_tensor("poolw", [128, 512], F32, kind="ExternalInput").ap()
    outT = nc.dram_tensor("outT", [2, 1024, 2048], F32, kind="ExternalOutput").ap()
    wsc = nc.dram_tensor("wsc", [NPIECE, 128, PW], BF16).ap()
    dbg_d = None
    if debug:
        dbg_d = nc.dram_tensor("dbg", [128, 4096], F32, kind="ExternalOutput").ap()

    S = Sched()
    st = ExitStack()
    with st:
        def sb(name, shape, dt):
            return st.enter_context(nc.sbuf_tensor(name, shape, dt))

        vec = sb("vec_sb", [128, 64], F32)
        nbg = sb("nbg", [128, 16], F32)
        ones_bf = sb("ones_bf", [128, 128], BF16)
        bd_bf = sb("bd_bf", [128, 128], BF16)
        olo_bf = sb("olo_bf", [128, 128], BF16)
        ohi_bf = sb("ohi_bf", [128, 128], BF16)
        zero_bf = sb("zero_bf", [128, 128], BF16)
        zero_rhs = sb("zero_rhs", [128, 512], BF16)
        band = sb("band_sb", [128, 12, 144], BF16)
        poolw = sb("poolw_sb", [128, 4, 128], BF16)
        cs1 = sb("cs1", [128, 2, 512], F32)
        cs2 = sb("cs2", [128, 2, 512], F32)
        kT2 = sb("kT2", [128, 2, 2048], BF16)
        Vpad = sb("Vpad", [128, 16, 2, 192], BF16)
        utok = sb("utok", [128, 16, 512], BF16)
        xkT = sb("xkT", [128, 8, 256], BF16)
        xv = sb("xv", [128, 2, 1024], BF16)
        hbuf = [sb("hA", [128, 8, 512], F32), sb("hB", [128, 8, 512], F32)]
        n_bf = sb("n_bf", [128, 8, 512], BF16)
        U = sb("U", [128, 24, 512], BF16)
        PT = sb("PT", [128, 3, 2, 512], BF16)
        NF = 8
        Fp = sb("Fp", [128, NF, 512], F32)
        SQ = sb("SQ", [128, 3, 512], BF16)
        rfin = sb("rfin", [128, 512], F32)
        pmT = sb("pmT", [128, 4, 512], BF16)
        wbuf = sb("wbuf", [128, 3, PW], BF16)
        ps = st.enter_context(nc.psum_tensor("ps", [128, 8, 512], F32))

        def ck(name):
            CK_LOG.append((name, sum(1 for o in S.ops if o["eng"] == "pe")))
            if stop_at is not None and name == stop_at:
                raise _Stop()

        class Rot:
            def __init__(self, n):
                self.n, self.i = n, 0

            def next(self):
                k = self.i % self.n
                self.i += 1
                return k

        frot, sqrot, ptrot = Rot(NF), Rot(3), Rot(3)
        bank_i = [0]

        def bank1():
            b = bank_i[0] % 8
            bank_i[0] += 1
            return b

        def bank2():
            if bank_i[0] % 2:
                bank_i[0] += 1
            b = bank_i[0] % 8
            bank_i[0] += 2
            return b

        def PS(b):
            return ("ps", b)

        def Fs():
            k = frot.next()
            return Fp[:, k, :], ("F", k)

        def SQs():
            k = sqrot.next()
            return SQ[:, k, :], ("SQ", k)

        def Us(k):
            return U[:, k, :], ("U", k)

        def mm(out, lhsT, rhs, start, stop, reads, writes):
            S.op("pe", lambda e: e.matmul(out, lhsT=lhsT, rhs=rhs, start=start, stop=stop), reads, writes)

        def act(out, in_, func, reads, writes, bias=None, scale=None):
            kw = {}
            if bias is not None:
                kw["bias"] = bias
            if scale is not None:
                kw["scale"] = scale
            S.op("act", lambda e: e.activation(out=out, in_=in_, func=func, **kw), reads, writes)

        def tt(eng, out, in0, in1, op, reads, writes):
            S.op(eng, lambda e: e.tensor_tensor(out=out, in0=in0, in1=in1, op=op), reads, writes)

        def stt(eng, out, in0, scalar, in1, op0, op1, reads, writes):
            S.op(eng, lambda e: e.scalar_tensor_tensor(out=out, in0=in0, scalar=scalar, in1=in1, op0=op0, op1=op1),
                 reads, writes)

        def ts(eng, out, in0, s1, op0, reads, writes, s2=None, op1=None):
            if op1 is None:
                S.op(eng, lambda e: e.tensor_scalar(out=out, in0=in0, scalar1=s1, scalar2=None, op0=op0), reads, writes)
            else:
                S.op(eng, lambda e: e.tensor_scalar(out=out, in0=in0, scalar1=s1, scalar2=s2, op0=op0, op1=op1),
                     reads, writes)

        def copy(eng, out, in_, reads, writes):
            if eng == "act":
                S.op("act", lambda e: e.activation(out=out, in_=in_, func=AF.Copy), reads, writes)
            else:
                S.op(eng, lambda e: e.tensor_copy(out=out, in_=in_), reads, writes)

        def recip(out, in_, reads, writes):
            S.op("dve", lambda e: e.reciprocal(out=out, in_=in_), reads, writes)

        dbg_col = [0]

        def dump(tag, ap, key, width):
            if debug and tag in debug:
                c0 = dbg_col[0]
                dbg_col[0] += width
                src = ap
                if ap.dtype != F32:
                    t, tk = Fs()
                    copy("dve", t[:, 0:width], ap, [key], [tk])
                    src, key2 = t[:, 0:width], tk
                else:
                    key2 = key
                S.dma("sp", lambda e: e.dma_start(out=dbg_d[:, c0:c0 + width], in_=src), "dbg", reads=[key2], writes=["dbgd"])

        piece_seq = []
        for s in range(n_seq):
            piece_seq += [PI_W1, PI_W1B, PI_XKV0, PI_XKV1]
            for tb in range(4):
                piece_seq += [PI_BLK0 + i for i in range(16)]
        pstate = dict(loaded=0, cur=-1)

        seen_blk = set()

        def next_piece(expect, ahead=2):
            pstate["cur"] += 1
            k = pstate["cur"]
            assert piece_seq[k] == expect, (k, piece_seq[k], expect)
            while pstate["loaded"] < min(k + 1 + ahead, len(piece_seq)):
                i = pstate["loaded"]
                slot = i % 3
                pid = piece_seq[i]
                if pid not in seen_blk:
                    S.dma("pool", lambda e, slot=slot, pid=pid: e.dma_start(out=wbuf[:, slot, :], in_=wst[pid]),
                          ("wc", slot), writes=[("wb", slot)])
                    seen_blk.add(pid)
                    if pid >= PI_BLK0:
                        S.dma("sp", lambda e, slot=slot, pid=pid: e.dma_start(out=wsc[pid], in_=wbuf[:, slot, :]),
                              ("wbk", slot), reads=[("wb", slot)], writes=[("wsc", pid)])
                else:
                    S.dma("sp", lambda e, slot=slot, pid=pid: e.dma_start(out=wbuf[:, slot, :], in_=wsc[pid]),
                          ("w", slot), reads=[("wsc", pid)], writes=[("wb", slot)])
                pstate["loaded"] += 1
            slot = k % 3
            return wbuf[:, slot, :], ("wb", slot)

        S.dma("sp", lambda e: e.dma_start(out=vec[:], in_=vec_d), "const", writes=["vec"])
        S.dma("pool", lambda e: e.dma_start(out=band[:], in_=band_d.rearrange("p (a n) -> p a n", n=144)), "constc",
              writes=["band"])
        S.dma("pool", lambda e: e.dma_start(out=poolw[:], in_=poolw_d.rearrange("p (g d) -> p g d", d=128)), "constc",
              writes=["poolw"])
        S.op("dve", lambda e: e.memset(ones_bf[:], 1.0), writes=["ones"])
        S.op("dve", lambda e: e.memset(zero_bf[:], 0.0), writes=["zero"])
        S.op("dve", lambda e: e.memset(zero_rhs[:], 0.0), writes=["zero"])
        S.op("dve", lambda e: e.memset(bd_bf[:], 0.0), writes=["bd"])
        S.op("dve", lambda e: e.memset(bd_bf[0:64, 0:64], 1.0), writes=["bd"])
        S.op("dve", lambda e: e.memset(bd_bf[64:128, 64:128], 1.0), writes=["bd"])
        S.op("dve", lambda e: e.memset(olo_bf[:], 0.0), writes=["olo"])
        S.op("dve", lambda e: e.memset(olo_bf[:, 0:64], 1.0), writes=["olo"])
        S.op("dve", lambda e: e.memset(ohi_bf[:], 0.0), writes=["ohi"])
        S.op("dve", lambda e: e.memset(ohi_bf[:, 64:128], 1.0), writes=["ohi"])
        ts("dve", nbg[:], vec[:, C_BG:C_BG + 16], -1.0, ALU.mult, ["vec"], ["nbg"])

        hcount = [0]
        sp_i = [0]
        out_keys = []

        def load_x(s, t0):
            hi = hcount[0] % 2
            hcount[0] += 1
            h = hbuf[hi]
            hk = [("h", hi, kc) for kc in range(8)]
            src = xT[s].rearrange("(kc p) t -> p kc t", p=128)[:, :, t0:t0 + 512]
            S.dma("sp", lambda e: e.dma_start(out=h[:], in_=src), ("x", hi), writes=hk)
            return h, hk

        def load_rope(cs, key, t0):
            S.dma("sp", lambda e: e.dma_start(out=cs[:], in_=rope_d.rearrange("a p t -> p a t")[:, :, t0:t0 + 512]),
                  key, writes=[key])

        def rstd_from(ss_bank, n, width=512, dest=None, expo=-0.5):
            lt, lk = Fs()
            act(lt[:, 0:width], ps[:, ss_bank, 0:width], AF.Ln, [PS(ss_bank)], [lk], bias=EPS, scale=1.0 / n)
            rt, rk = Fs() if dest is None else dest
            act(rt[:, 0:width], lt[:, 0:width], AF.Exp, [lk], [rk], scale=expo)
            return rt, rk

        def rmsnorm(h, hk, gcol, out_fn, width=512, nchunk=8):
            rr = rms_stats(h, hk, width, nchunk)
            rms_apply(h, hk, gcol, out_fn, rr, width, nchunk)

        def rms_stats(h, hk, width=512, nchunk=8, dest=None, expo=-0.5):
            b = bank1()
            for kc in range(nchunk):
                sq, sk = SQs()
                if kc % 2 == 0:
                    tt("pool", sq[:, 0:width], h[:, kc, 0:width], h[:, kc, 0:width], ALU.mult, [hk[kc]], [sk])
                else:
                    act(sq[:, 0:width], h[:, kc, 0:width], AF.Square, [hk[kc]], [sk])
                mm(ps[:, b, 0:width], ones_bf[:], sq[:, 0:width], kc == 0, kc == nchunk - 1, [sk, "ones"], [PS(b)])
            return rstd_from(b, 1024.0, width, dest, expo)

        def scale_g(h, hk, gcol):
            for kc in range(8):
                if kc % 2 == 0:
                    act(n_bf[:, kc, :], h[:, kc, :], AF.Copy, [hk[kc], "vec"], [("n", kc)], scale=vec[:, gcol + kc:gcol + kc + 1])
                else:
                    ts("dve", n_bf[:, kc, :], h[:, kc, :], vec[:, gcol + kc:gcol + kc + 1], ALU.mult, [hk[kc], "vec"],
                       [("n", kc)])

        def rms_apply(h, hk, gcol, out_fn, rr, width=512, nchunk=8):
            rt, rk = rr
            for kc in range(nchunk):
                o, ok = out_fn(kc)
                stt("dve", o, h[:, kc, 0:width], vec[:, gcol + kc:gcol + kc + 1], rt[:, 0:width], ALU.mult, ALU.mult,
                    [hk[kc], rk, "vec"], [ok])

        def nbf_out(kc):
            return n_bf[:, kc, :], ("n", kc)

        NKEYS = [("n", kc) for kc in range(8)]

        def qk_chunk(wp, wk, off_a, off_s, gcol, gscol, cs, cskey, dst, dstkey, nfn=None):
            ba, bs_ = bank1(), bank1()
            if nfn is None:
                nfn = nbf_out
            for kc in range(8):
                na, nk = nfn(kc)
                mm(ps[:, ba, :], wp[:, off_a + kc * 128: off_a + (kc + 1) * 128], na, kc == 0, kc == 7,
                   [wk, nk], [PS(ba)])
            for kc in range(8):
                na, nk = nfn(kc)
                mm(ps[:, bs_, :], wp[:, off_s + kc * 128: off_s + (kc + 1) * 128], na, kc == 0, kc == 7,
                   [wk, nk], [PS(bs_)])
            ck("qk_mm")
            sq, sk = SQs()
            act(sq, ps[:, ba, :], AF.Square, [PS(ba)], [sk])
            ck("qk_sq")
            bss = bank1()
            mm(ps[:, bss, :], bd_bf[:], sq, True, True, [sk, "bd"], [PS(bss)])
            rt, rk = rstd_from(bss, 64.0)
            ck("qk_rstd")
            t1, k1 = Fs()
            stt("dve", t1, ps[:, ba, :], vec[:, gcol:gcol + 1], cs[:, 0, :], ALU.mult, ALU.mult, [PS(ba), "vec", cskey], [k1])
            ck("qk_t1")
            t2, k2 = Fs()
            stt("dve", t2, ps[:, bs_, :], vec[:, gscol:gscol + 1], cs[:, 1, :], ALU.mult, ALU.mult, [PS(bs_), "vec", cskey], [k2])
            tt("pool", t1, t1, t2, ALU.add, [k1, k2], [k1])
            ck("qk_t3")
            tt("pool", dst, t1, rt, ALU.mult, [k1, rk], [dstkey])

        p1_early = set()

        def p1_n(tb):
            if tb % 2 == 0:
                return (lambda kc: (n_bf[:, kc, :], ("n", kc)))
            return (lambda kc: (U[:, kc, :], ("U", kc)))

        def p1_front(s, tb):
            h, hk = load_x(s, tb * 512)
            load_rope(cs1, "cs1", tb * 512)
            rmsnorm(h, hk, C_GMIX, p1_n(tb))

        def front_a(s, tb, pre=None):
            t0 = tb * 512
            h, hk = pre if pre is not None else load_x(s, t0)
            load_rope(cs2, "cs2", t0)
            rmsnorm(h, hk, C_GMIX, nbf_out)
            return dict(h=h, hk=hk, t0=t0, s=s, tb=tb)

        def front_b(ctx):
            wp, wk = next_piece(PI_BLK0 + 0)
            for c in range(4):
                qd, qk_ = Us(c)
                qk_chunk(wp, wk, c * 1024, (4 + c) * 1024, C_GQ, C_GQS, cs2, "cs2", qd, qk_)
            ck("q")

        def pool_a(tb):
            t0 = tb * 512
            for g in range(4):
                b = bank1()
                mm(ps[:, b, :], zero_bf[:], zero_rhs[:], True, False, ["zero"], [PS(b)])
                tts = [x for x in range(tb * 4 - 1, tb * 4 + 5) if 0 <= x < 16]
                for i, ttk in enumerate(tts):
                    n0 = 128 * ttk - 8 - t0
                    lo, hi_ = max(n0, 0), min(n0 + 144, 512)
                    var = 0 if ttk == 0 else (2 if ttk == 15 else 1)
                    mm(ps[:, b, lo:hi_], utok[:, ttk, g * 128:(g + 1) * 128], band[:, var * 4 + g, lo - n0:hi_ - n0],
                       False, i == len(tts) - 1, [("ut", ttk), "band"], [PS(b)])
                pd, pk = Us(8 + g)
                copy("act", pd, ps[:, b, :], [PS(b)], [pk])

        def pool_b():
            for g in range(4):
                pd, pk = Us(8 + g)
                b2 = bank1()
                mm(ps[:, b2, :], poolw[:, g, :], pd, True, True, ["poolw", pk], [PS(b2)])
                ts("dve", pmT[:, g, :], ps[:, b2, :], vec[:, C_PS + g:C_PS + g + 1], ALU.mult, [PS(b2), "vec"], [("pm", g)])

        def attention():
            items = [(c, kt) for c in range(4) for kt in range(16)]
            r_of = {}

            def qk(i):
                c, kt = items[i]
                kvh = c // 2
                qd, qk_ = Us(c)
                bS = 2 * (i % 2)
                for e_ in range(2):
                    mm(ps[:, bS + e_, :], kT2[e_ * 64:(e_ + 1) * 64, kvh, kt * 128:(kt + 1) * 128],
                       qd[e_ * 64:(e_ + 1) * 64, :], True, True, [("kT", kvh, kt // 4), qk_], [PS(bS + e_)])

            def ex(i):
                bS = 2 * (i % 2)
                r = ptrot.next()
                r_of[i] = r
                act(PT[:, r, :, :], ps[:, bS:bS + 2, :], AF.Exp, [PS(bS), PS(bS + 1)], [("PT", r)], scale=0.125)

            def pv(i):
                c, kt = items[i]
                kvh = c // 2
                r = r_of[i]
                bE, bO = (4, 5) if c % 2 == 0 else (6, 7)
                mm(ps[:, bE, :], Vpad[:, kt, kvh, 64:192], PT[:, r, 0, :], kt == 0, kt == 15,
                   [("V", kt), "Vpad", ("PT", r)], [PS(bE)])
                mm(ps[:, bO, :], Vpad[:, kt, kvh, 0:128], PT[:, r, 1, :], kt == 0, kt == 15,
                   [("V", kt), "Vpad", ("PT", r)], [PS(bO)])
                if kt == 15:
                    rc, rck = Fs()
                    recip(rc[64:128, :], ps[64:128, bE, :], [PS(bE)], [rck])
                    recip(rc[0:64, :], ps[0:64, bO, :], [PS(bO)], [rck])
                    od, ok_ = Us(4 + c)
                    tt("dve", od[0:64, :], ps[0:64, bE, :], rc[64:128, :], ALU.mult, [PS(bE), rck], [ok_])
                    tt("dve", od[64:128, :], ps[64:128, bO, :], rc[0:64, :], ALU.mult, [PS(bO), rck], [ok_])

            qk(0)
            for i in range(len(items)):
                ex(i)
                if i + 1 < len(items):
                    qk(i + 1)
                pv(i)
            bank_i[0] = 0

        def back(s, tb, ctx):
            h, hk, t0 = ctx["h"], ctx["hk"], ctx["t0"]
            dbg0 = debug and tb == 0 and s == 0
            if dbg0:
                dump("qT", U[:, 0, :], ("U", 0), 512)
            pre = load_x(s, (tb + 1) * 512) if tb < 3 else None
            if s == 0 and tb == 1 and n_seq > 1:
                for pid in (PI_W1, PI_W1B, PI_XKV0, PI_XKV1):
                    S.dma("pool", lambda e, pid=pid: e.dma_start(out=wsc[pid], in_=wst[pid]), ("cast", pid),
                          writes=[("wsc", pid)])
            attention()
            if dbg0:
                dump("oT", U[:, 4, :], ("U", 4), 512)
            ck("attn")
            if tb == 0:
                pool_a(tb)
                pool_b()
            if dbg0:
                dump("pm", pmT[:, 0, :], ("pm", 0), 512)
            ck("pool")
            for jp in range(4):
                wp, wk = next_piece(PI_BLK0 + 1 + jp)
                gslots = []
                for jj in range(2):
                    j = jp * 2 + jj
                    base = jj * 24 * 128
                    bga, bgp = bank1(), bank1()
                    for kc in range(8):
                        mm(ps[:, bga, :], wp[:, base + kc * 128: base + (kc + 1) * 128], n_bf[:, kc, :], kc == 0, kc == 7,
                           [wk, ("n", kc)], [PS(bga)])
                    for kc in range(8):
                        mm(ps[:, bgp, :], wp[:, base + (8 + kc) * 128: base + (9 + kc) * 128], n_bf[:, kc, :], kc == 0,
                           kc == 7, [wk, ("n", kc)], [PS(bgp)])
                    ea, eak = Fs()
                    act(ea, ps[:, bga, :], AF.Exp, [PS(bga), "nbg"], [eak], bias=nbg[:, j:j + 1], scale=-1.0)
                    ep, epk = Fs()
                    act(ep, ps[:, bgp, :], AF.Exp, [PS(bgp), "nbg"], [epk], bias=nbg[:, 8 + j:9 + j], scale=-1.0)
                    act(ea, ea, AF.Ln, [eak], [eak], bias=1.0, scale=1.0)
                    act(ep, ep, AF.Ln, [epk], [epk], bias=1.0, scale=1.0)
                    act(ea, ea, AF.Exp, [eak], [eak], scale=-1.0)
                    act(ep, ep, AF.Exp, [epk], [epk], scale=-1.0)
                    gslots.append((ea, eak, ep, epk))
                for jj in range(2):
                    j = jp * 2 + jj
                    base = jj * 24 * 128
                    ea, eak, ep, epk = gslots[jj]
                    ba, bp = bank1(), bank1()
                    for kc in range(4):
                        mm(ps[:, ba, :], wp[:, base + (16 + kc) * 128: base + (17 + kc) * 128], U[:, 4 + kc, :], kc == 0,
                           kc == 3, [wk, ("U", 4 + kc)], [PS(ba)])
                    for kc in range(4):
                        mm(ps[:, bp, :], wp[:, base + (20 + kc) * 128: base + (21 + kc) * 128], pmT[:, kc, :], kc == 0,
                           kc == 3, [wk, ("pm", kc)], [PS(bp)])
                    tt("dve", ea, ps[:, ba, :], ea, ALU.mult, [PS(ba), eak], [eak])
                    tt("dve", ep, ps[:, bp, :], ep, ALU.mult, [PS(bp), epk], [epk])
                    xd, xk_ = Us(16 + j)
                    tt("pool", xd, ea, ep, ALU.add, [eak, epk], [xk_])
            wp, wk = next_piece(PI_BLK0 + 5)
            for j in range(8):
                b = bank1()
                for kc in range(8):
                    mm(ps[:, b, :], wp[:, (j * 8 + kc) * 128:(j * 8 + kc + 1) * 128], U[:, 16 + kc, :], kc == 0, kc == 7,
                       [wk, ("U", 16 + kc)], [PS(b)])
                tt("dve", h[:, j, :], ps[:, b, :], h[:, j, :], ALU.add, [PS(b), hk[j]], [hk[j]])
            if dbg0:
                dump("h1", h[:, 0, :], hk[0], 512)
            ck("h1")
            scale_g(h, hk, C_GCROSS)
            r2t, r2k = rms_stats(h, hk)
            if tb < 3:
                pool_a(tb + 1)
            wp, wk = next_piece(PI_BLK0 + 6)
            for j in range(8):
                b = bank1()
                for kc in range(8):
                    mm(ps[:, b, :], wp[:, (j * 8 + kc) * 128:(j * 8 + kc + 1) * 128], n_bf[:, kc, :], kc == 0, kc == 7,
                       [wk, ("n", kc)], [PS(b)])
                xd, xk_ = Us(j)
                tt("dve", xd, ps[:, b, :], r2t, ALU.mult, [PS(b), r2k], [xk_])
            if tb < 3:
                pool_b()
            xs = {}

            def x_s(xh):
                bS = bank2()
                xs[xh] = bS
                for mt in range(2):
                    for dc in range(2):
                        mm(ps[:, bS + mt, :], xkT[:, 2 * xh + dc, mt * 128:(mt + 1) * 128], U[:, 2 * xh + dc, :],
                           dc == 0, dc == 1, [("xk", 2 * xh + dc), ("U", 2 * xh + dc)], [PS(bS + mt)])

            def x_e(xh):
                bS = xs[xh]
                r = ptrot.next()
                xs[("r", xh)] = r
                act(PT[:, r, :, :], ps[:, bS:bS + 2, :], AF.Exp, [PS(bS), PS(bS + 1)], [("PT", r)], scale=1.0 / 16.0)

            def x_pv(xh):
                r = xs[("r", xh)]
                bsum = bank1()
                for mt in range(2):
                    mm(ps[:, bsum, :], ones_bf[:], PT[:, r, mt, :], mt == 0, mt == 1, ["ones", ("PT", r)], [PS(bsum)])
                rc, rck = Fs()
                act(rc, ps[:, bsum, :], AF.Ln, [PS(bsum)], [rck])
                act(rc, rc, AF.Exp, [rck], [rck], scale=-1.0)
                for dc in range(2):
                    b = bank1()
                    for mt in range(2):
                        mm(ps[:, b, :], xv[:, mt, (2 * xh + dc) * 128:(2 * xh + dc + 1) * 128], PT[:, r, mt, :],
                           mt == 0, mt == 1, [("xv", mt), ("PT", r)], [PS(b)])
                    od, ok_ = Us(8 + 2 * xh + dc)
                    tt("dve", od, ps[:, b, :], rc, ALU.mult, [PS(b), rck], [ok_])

            x_s(0)
            for xh in range(4):
                x_e(xh)
                if xh + 1 < 4:
                    x_s(xh + 1)
                x_pv(xh)
            wp, wk = next_piece(PI_BLK0 + 7)
            for j in range(8):
                b = bank1()
                for kc in range(8):
                    mm(ps[:, b, :], wp[:, (j * 8 + kc) * 128:(j * 8 + kc + 1) * 128], U[:, 8 + kc, :], kc == 0, kc == 7,
                       [wk, ("U", 8 + kc)], [PS(b)])
                tt("dve", h[:, j, :], ps[:, b, :], h[:, j, :], ALU.add, [PS(b), hk[j]], [hk[j]])
            if dbg0:
                dump("h2", h[:, 0, :], hk[0], 512)
            ck("h2")
            scale_g(h, hk, C_GFFN)
            r3t, r3k = rms_stats(h, hk, dest=(rfin[:], "rfin"), expo=-1.0)
            nctx = None
            for p_ in range(2):
                for half in range(2):
                    wp, wk = next_piece(PI_BLK0 + 8 + p_ * 4 + half)
                    for i8 in range(8):
                        i = half * 8 + i8
                        b = bank1()
                        for kc in range(8):
                            mm(ps[:, b, :], wp[:, (i8 * 8 + kc) * 128:(i8 * 8 + kc + 1) * 128], n_bf[:, kc, :], kc == 0,
                               kc == 7, [wk, ("n", kc)], [PS(b)])
                        rl, rlk = Fs()
                        if i % 2 == 0:
                            act(rl, ps[:, b, :], AF.Relu, [PS(b)], [rlk])
                        else:
                            ts("dve", rl, ps[:, b, :], 0.0, ALU.max, [PS(b)], [rlk])
                        slot = (p_ * 16 + i) % 24
                        hd, hdk = Us(slot)
                        tt("pool", hd, rl, rl, ALU.mult, [rlk], [hdk])
                if p_ == 1 and tb < 3:
                    nctx = front_a(s, tb + 1, pre)
                elif p_ == 1 and tb == 3 and s + 1 < n_seq:
                    p1_front(s + 1, 0)
                    p1_early.add(s + 1)
                for jh in range(2):
                    wp, wk = next_piece(PI_BLK0 + 10 + p_ * 4 + jh)
                    for j4 in range(4):
                        j = jh * 4 + j4
                        b = bank1()
                        for kc in range(16):
                            slot = (p_ * 16 + kc) % 24
                            mm(ps[:, b, :], wp[:, (j4 * 16 + kc) * 128:(j4 * 16 + kc + 1) * 128], U[:, slot, :], kc == 0,
                               kc == 15, [wk, ("U", slot)], [PS(b)])
                        ft, fk = Fs()
                        tt("dve", ft, ps[:, b, :], r3t, ALU.mult, [PS(b), r3k], [fk])
                        tt("dve", h[:, j, :], h[:, j, :], ft, ALU.add, [hk[j], fk], [hk[j]])
            ck("ffn")
            rr = rms_stats(h, hk, dest=(rfin[:], "rfin"))
            if nctx is not None:
                front_b(nctx)
            rms_apply(h, hk, C_GFINAL, lambda kc: (h[:, kc, :], hk[kc]), rr)
            ck("fnorm")
            dst = outT[s].rearrange("(kc p) t -> p kc t", p=128)[:, :, t0:t0 + 512]
            S.dma("sp", lambda e: e.dma_start(out=dst, in_=h[:]), ("o", hk[0][1]), reads=hk, writes=[("outT", s, tb)])
            out_keys.append(("outT", s, tb))
            ck("blk")
            return nctx

        try:
          for s in range(n_seq):
              wp, wk = next_piece(PI_W1, ahead=1)
              wpb, wkb = next_piece(PI_W1B, ahead=1)
              if s == 0:
                  S.op("pool", lambda e: e.memset(Vpad[:], 1.0), writes=["Vpad"])
              mnv = [U[:, 8 + kc // 2, (kc % 2) * 256:(kc % 2) * 256 + 256] for kc in range(8)]
              mnk = [("U", 8 + kc // 2) for kc in range(8)]
              memh = {}

              def k_stage(tb):
                  nfn = p1_n(tb)
                  for kvh in range(2):
                      qk_chunk(wp, wk, kvh * 2048, kvh * 2048 + 1024, C_GK, C_GKS, cs1, "cs1",
                               kT2[:, kvh, tb * 512:tb * 512 + 512], ("kT", kvh, tb), nfn)

              def v_stage(tb):
                  nfn = p1_n(tb)
                  for tt_ in range(4):
                      kt = tb * 4 + tt_
                      bv = bank1()
                      for kc in range(8):
                          na, nk = nfn(kc)
                          mm(ps[:, bv, 0:128], na[:, tt_ * 128:(tt_ + 1) * 128],
                             wp[:, 4096 + kc * 128: 4096 + (kc + 1) * 128], kc == 0, kc == 7, [wk, nk], [PS(bv)])
                      copy("act", Vpad[:, kt, :, 64:128], ps[:, bv, 0:128].rearrange("p (a d) -> p a d", d=64),
                           [PS(bv), "Vpad"], [("V", kt)])

              def u_stage(tb):
                  nfn = p1_n(tb)
                  for tt_ in range(4):
                      kt = tb * 4 + tt_
                      bu = bank1()
                      for kc in range(8):
                          na, nk = nfn(kc)
                          mm(ps[:, bu, :], na[:, tt_ * 128:(tt_ + 1) * 128], wpb[:, kc * 512:(kc + 1) * 512],
                             kc == 0, kc == 7, [wkb, nk], [PS(bu)])
                      copy("dve", utok[:, kt, :], ps[:, bu, :], [PS(bu)], [("ut", kt)])

              def mem_load():
                  hi = hcount[0] % 2
                  hcount[0] += 1
                  hm = hbuf[hi]
                  hmk = [("h", hi, kc) for kc in range(8)]
                  S.dma("sp", lambda e, hm=hm, s=s: e.dma_start(out=hm[:, :, 0:256],
                                                                  in_=memT[s].rearrange("(kc p) t -> p kc t", p=128)),
                        ("x", hi), writes=hmk)
                  memh["h"], memh["k"] = hm, hmk

              def mem_norm():
                  rmsnorm(memh["h"], memh["k"], C_GMEM, lambda kc: (mnv[kc], mnk[kc]), width=256)

              def xk_stage(wx, wxk):
                  for j in range(8):
                      b = bank1()
                      for kc in range(8):
                          mm(ps[:, b, 0:256], wx[:, (j * 8 + kc) * 128:(j * 8 + kc + 1) * 128], mnv[kc], kc == 0, kc == 7,
                             [wxk, mnk[kc]], [PS(b)])
                      copy("act" if j % 2 else "dve", xkT[:, j, :], ps[:, b, 0:256], [PS(b)], [("xk", j)])

              def xv_stage(wx, wxk):
                  for mt in range(2):
                      for jq in range(2):
                          b = bank1()
                          for jj in range(4):
                              j = jq * 4 + jj
                              for kc in range(8):
                                  mm(ps[:, b, jj * 128:(jj + 1) * 128], mnv[kc][:, mt * 128:(mt + 1) * 128],
                                     wx[:, (j * 8 + kc) * 128:(j * 8 + kc + 1) * 128], kc == 0, kc == 7, [wxk, mnk[kc]],
                                     [PS(b)])
                          copy("act" if jq % 2 else "dve", xv[:, mt, jq * 512:(jq + 1) * 512], ps[:, b, :], [PS(b)],
                               [("xv", mt)])

              if s not in p1_early:
                  p1_front(s, 0)
              for tb in range(3):
                  ck("p1_norm")
                  k_stage(tb)
                  ck("p1_k")
                  p1_front(s, tb + 1)
                  if tb == 2:
                      mem_load()
                  v_stage(tb)
                  u_stage(tb)
                  ck("p1_blk")
              k_stage(3)
              v_stage(3)
              wx0, wx0k = next_piece(PI_XKV0, ahead=1)
              mem_norm()
              u_stage(3)
              if debug:
                  dump("kT", kT2[:, 0, 0:512], ("kT", 0, 0), 512)
                  dump("V", Vpad[:, 0, 0, 64:128], ("V", 0), 64)
                  dump("ut", utok[:, 0, :], ("ut", 0), 512)
              ck("p1")
              xk_stage(wx0, wx0k)
              wx1, wx1k = next_piece(PI_XKV1)
              ctx = front_a(s, 0)
              xv_stage(wx1, wx1k)
              if debug:
                  dump("xk", xkT[:, 0, :], ("xk", 0), 256)
                  dump("xv", xv[:, 0, 0:512], ("xv", 0), 512)
              ck("mem")
              front_b(ctx)
              for tb in range(4):
                  ctx = back(s, tb, ctx)
        except _Stop:
            pass
        S.op("sp", lambda e: e.nop(), reads=out_keys + ["dbgd"])
        S.emit(nc, st)
    return nc


def _cm(w):
    K, N = w.shape
    return w.reshape(K // 128, 128, N // 128, 128).transpose(2, 1, 0, 3)


def _perm64():
    perm = np.zeros(64, np.int64)
    perm_sw = np.zeros(64, np.int64)
    for half in range(2):
        for axis in range(2):
            for f in range(16):
                ip = half * 32 + axis * 16 + f
                perm[ip] = axis * 32 + half * 16 + f
                perm_sw[ip] = axis * 32 + (1 - half) * 16 + f
    return perm, perm_sw


def _pack_weights(w_in, w_attn_up, w_pool_up, w_out, w_xq, w_xkv, w_xo, w_ff1, w_ff2):
    perm, perm_sw = _perm64()
    wst = np.zeros((NPIECE, 128, PW), np.float32)

    def flat(a):
        return np.ascontiguousarray(a).reshape(128, -1)

    def put(pid, off, a):
        a = flat(a)
        wst[pid, :, off:off + a.shape[1]] = a
        return off + a.shape[1]

    off = 0
    for kvh in range(2):
        cols = 512 + kvh * 64 + perm
        cols_sw = 512 + kvh * 64 + perm_sw
        for cc in (cols, cols_sw):
            wk = w_in[:, np.concatenate([cc, cc])]
            off = put(PI_W1, off, _cm(wk)[0])
    off = put(PI_W1, off, _cm(w_in[:, 640:768])[0])
    put(PI_W1B, 0, w_in[:, 768:1280].reshape(8, 128, 512).transpose(1, 0, 2))
    cx = _cm(w_xkv)
    put(PI_XKV0, 0, cx[0:8].transpose(1, 0, 2, 3))
    put(PI_XKV1, 0, cx[8:16].transpose(1, 0, 2, 3))
    qcols = np.concatenate([h * 64 + perm for h in range(8)])
    qscols = np.concatenate([h * 64 + perm_sw for h in range(8)])
    off = put(PI_BLK0 + 0, 0, _cm(w_in[:, qcols]).transpose(1, 0, 2, 3))
    put(PI_BLK0 + 0, off, _cm(w_in[:, qscols]).transpose(1, 0, 2, 3))
    cga = _cm(w_in[:, 1280:2304])
    cgp = _cm(w_in[:, 2304:3328])
    cau = _cm(w_attn_up)
    cpu = _cm(w_pool_up)
    for jp in range(4):
        off = 0
        for jj in range(2):
            j = jp * 2 + jj
            off = put(PI_BLK0 + 1 + jp, off, cga[j])
            off = put(PI_BLK0 + 1 + jp, off, cgp[j])
            off = put(PI_BLK0 + 1 + jp, off, cau[j])
            off = put(PI_BLK0 + 1 + jp, off, cpu[j])
    put(PI_BLK0 + 5, 0, _cm(w_out).transpose(1, 0, 2, 3))
    put(PI_BLK0 + 6, 0, _cm(w_xq).transpose(1, 0, 2, 3))
    put(PI_BLK0 + 7, 0, _cm(w_xo).transpose(1, 0, 2, 3))
    c1 = _cm(w_ff1)
    c2 = _cm(w_ff2)
    for p_ in range(2):
        for half in range(2):
            i0 = p_ * 16 + half * 8
            put(PI_BLK0 + 8 + p_ * 4 + half, 0, c1[i0:i0 + 8].transpose(1, 0, 2, 3))
        for jh in range(2):
            put(PI_BLK0 + 10 + p_ * 4 + jh, 0, c2[jh * 4:(jh + 1) * 4, :, p_ * 16:(p_ + 1) * 16, :].transpose(1, 0, 2, 3))
    return wst


def _const_tables():
    S_, GW = 2048, 64
    t = np.arange(S_)
    posn = np.stack([t // GW, t % GW], 0).astype(np.float32)
    inv = (10000.0 ** (-np.arange(0, 32, 2, dtype=np.float32) / 32.0)).astype(np.float32)
    rope = np.zeros((2, 128, S_), np.float32)
    for hh in range(2):
        for half in range(2):
            for axis in range(2):
                for f in range(16):
                    p = hh * 64 + half * 32 + axis * 16 + f
                    ang = (posn[axis] * inv[f]).astype(np.float32)
                    rope[0, p] = np.cos(ang)
                    rope[1, p] = np.sin(ang) * (-1.0 if half == 0 else 1.0)
    band = np.zeros((128, 3, 4, 144), np.float32)
    for var, ttk in enumerate((0, 5, 15)):
        for g, w in enumerate((2, 4, 8, 16)):
            for n in range(-8, 136):
                tp = 128 * ttk + n
                if tp < 0 or tp >= S_:
                    continue
                lo = min(max(tp - w // 2, 0), S_)
                hi = min(max(tp + (w - w // 2), 0), S_)
                cnt = float(hi - lo)
                for k in range(128):
                    tk = 128 * ttk + k
                    v = 0.0
                    if lo <= tk < hi:
                        v += 1.0 / cnt
                    if tk == tp:
                        v -= 1.0
                    band[k, var, g, n + 8] = v
    return rope, band.reshape(128, 12 * 144)


def _prep(inputs):
    g = lambda k: np.asarray(inputs[k], dtype=np.float32)
    x, mem = g("x"), g("mem")
    perm, perm_sw = _perm64()
    wst = _pack_weights(g("w_in")[0], g("w_attn_up")[0], g("w_pool_up")[0], g("w_out")[0], g("w_xq")[0],
                        g("w_xkv")[0], g("w_xo")[0], g("w_ff1")[0], g("w_ff2")[0])
    vec = np.zeros((128, 64), np.float32)
    for c0, name in ((C_GMIX, "g_mix"), (C_GCROSS, "g_cross"), (C_GFFN, "g_ffn"), (C_GMEM, "g_mem")):
        vec[:, c0:c0 + 8] = g(name)[0].reshape(8, 128).T
    vec[:, C_GFINAL:C_GFINAL + 8] = g("g_final").reshape(8, 128).T
    vec[:, C_BG:C_BG + 16] = g("b_gate")[0].reshape(16, 128).T
    vec[:, C_PS:C_PS + 4] = g("pool_scale")[0].reshape(4, 128).T
    gq, gk = g("g_q")[0], g("g_k")[0]
    vec[:, C_GQ] = np.tile(gq[perm], 2)
    vec[:, C_GQS] = np.tile(gq[perm_sw], 2)
    vec[:, C_GK] = np.tile(gk[perm], 2)
    vec[:, C_GKS] = np.tile(gk[perm_sw], 2)
    poolw = np.ascontiguousarray(g("pool_w")[0].transpose(1, 0, 2)).reshape(128, 512)
    rope, band = _const_tables()
    xT = np.ascontiguousarray(x.transpose(0, 2, 1))
    memT = np.ascontiguousarray(mem.transpose(0, 2, 1))
    in_maps = []
    for c in range(8):
        in_maps.append(dict(xT=xT[2 * c:2 * c + 2], memT=memT[2 * c:2 * c + 2], wst=wst, vec=vec, rope=rope,
                            band=band, poolw=poolw))
    return in_maps


def kernel(**inputs):
    in_maps = _prep(inputs)
    nc = build_core()
    res = run_bass_kernel_spmd(nc, in_maps, core_ids=list(range(8)))
    outT = np.stack([r["outT"] for r in res.results], 0)
    out = outT.reshape(16, 1024, 2048).transpose(0, 2, 1)
    return np.ascontiguousarray(out).astype(np.float32)
```

```python
import numpy as np
from contextlib import ExitStack
import concourse.bass as bass
import concourse.mybir as mybir
from concourse.bass_utils import run_bass_kernel_spmd

F32 = mybir.dt.float32
BF16 = mybir.dt.bfloat16
ALU = mybir.AluOpType
AF = mybir.ActivationFunctionType

ENGS = ("pe", "act", "dve", "pool", "sp")
EPS = 1e-6
NPIECE = 20
PW = 8192


class Sched:
    SAME_ENG_DIST = 10 ** 9
    SEM_CAP = 20000

    def __init__(self):
        self.ops = []

    def op(self, eng, fn, reads=(), writes=()):
        isps = lambda k: isinstance(k, tuple) and k[0] == "ps"
        w = list(writes) + [k for k in reads if isps(k)]
        r = [k for k in reads if not isps(k)]
        self.ops.append(dict(eng=eng, fn=fn, r=tuple(r), w=tuple(w), dma=None))

    def dma(self, q, fn, semkey, reads=(), writes=(), after=()):
        self.ops.append(dict(eng=q, fn=fn, r=tuple(reads), w=tuple(writes), dma=semkey, after=tuple(after)))

    def analyze(self):
        last_w, readers = {}, {}
        pos = {e: 0 for e in ENGS}
        dma_cnt = {}
        for i, o in enumerate(self.ops):
            deps = set()
            for r in o["r"] + o.get("after", ()):
                if r in last_w:
                    deps.add(last_w[r])
            for w in o["w"]:
                if w in last_w:
                    deps.add(last_w[w])
                deps.update(readers.get(w, ()))
            deps.discard(i)
            o["pos"] = pos[o["eng"]]
            pos[o["eng"]] += 1
            dma_waits, eng_deps = {}, {}
            for d in deps:
                od = self.ops[d]
                if od["dma"] is not None:
                    k = od["dma"]
                    dma_waits[k] = dma_cnt[k] * 16
                else:
                    e2 = od["eng"]
                    if e2 == o["eng"]:
                        if e2 == "pe":
                            continue
                        if o["pos"] - od["pos"] > self.SAME_ENG_DIST:
                            continue
                    if e2 not in eng_deps or self.ops[eng_deps[e2]]["pos"] < od["pos"]:
                        eng_deps[e2] = d
            o["dma_waits"] = dma_waits
            o["eng_deps"] = eng_deps
            for d in eng_deps.values():
                self.ops[d]["ms"] = True
            if o["dma"] is not None:
                dma_cnt[o["dma"]] = dma_cnt.get(o["dma"], 0) + 1
            for r in o["r"]:
                readers.setdefault(r, []).append(i)
            for w in o["w"]:
                last_w[w] = i
                readers[w] = []
        cnt = {e: 0 for e in ENGS}
        for o in self.ops:
            if o.get("ms"):
                o["msidx"] = cnt[o["eng"]]
                cnt[o["eng"]] += 1
        self.ms_count = cnt
        self.dma_keys = list(dma_cnt.keys())
        self.dma_total = dict(dma_cnt)

    def emit(self, nc, stack):
        self.analyze()
        CAP = self.SEM_CAP
        esems = {}
        for e in ENGS:
            n = max(1, (self.ms_count[e] + CAP - 1) // CAP)
            esems[e] = [stack.enter_context(nc.semaphore(f"s_{e}{k}")) for k in range(n)]
        dsems = {k: stack.enter_context(nc.semaphore(f"d_{k}")) for k in self.dma_keys}
        block = stack.enter_context(nc.Block())
        ops = self.ops

        def run(ename, eng):
            waited = {}
            for o in ops:
                if o["eng"] != ename:
                    continue
                for k, v in o["dma_waits"].items():
                    if waited.get(("d", k), 0) < v:
                        eng.wait_ge(dsems[k], v)
                        waited[("d", k)] = v
                for e2, d in o["eng_deps"].items():
                    mi = ops[d]["msidx"]
                    si, v = mi // CAP, mi % CAP + 1
                    if waited.get((e2, si), 0) < v:
                        eng.wait_ge(esems[e2][si], v)
                        waited[(e2, si)] = v
                ins = o["fn"](eng)
                if o["dma"] is not None:
                    ins.then_inc(dsems[o["dma"]], 16)
                elif o.get("ms"):
                    ins.then_inc(esems[ename][o["msidx"] // CAP], 1)

        @block.tensor
        def _(eng):
            run("pe", eng)

        @block.scalar
        def _(eng):
            run("act", eng)

        @block.vector
        def _(eng):
            run("dve", eng)

        @block.gpsimd
        def _(eng):
            run("pool", eng)

        @block.sync
        def _(eng):
            run("sp", eng)
            for k in self.dma_keys:
                eng.wait_ge(dsems[k], self.dma_total[k] * 16)


C_GMIX, C_GCROSS, C_GFFN, C_GFINAL, C_GMEM, C_BG, C_PS, C_GQ, C_GQS, C_GK, C_GKS = 0, 8, 16, 24, 32, 40, 56, 60, 61, 62, 63
PI_W1, PI_W1B, PI_XKV0, PI_XKV1, PI_BLK0 = 0, 1, 2, 3, 4


CK_LOG = []


class _Stop(Exception):
    pass


def build_core(n_seq=2, debug=None, stop_at=None):
    nc = bass.Bass("TRN2", target_bir_lowering=False)
    xT = nc.dram_tensor("xT", [2, 1024, 2048], F32, kind="ExternalInput").ap()
    memT = nc.dram_tensor("memT", [2, 1024, 256], F32, kind="ExternalInput").ap()
    wst = nc.dram_tensor("wst", [NPIECE, 128, PW], F32, kind="ExternalInput").ap()
    vec_d = nc.dram_tensor("vec", [128, 64], F32, kind="ExternalInput").ap()
    rope_d = nc.dram_tensor("rope", [2, 128, 2048], F32, kind="ExternalInput").ap()
    band_d = nc.dram_tensor("band", [128, 12 * 144], F32, kind="ExternalInput").ap()
    poolw_d = nc.dram_tensor("poolw", [128, 512], F32, kind="ExternalInput").ap()
    outT = nc.dram_tensor("outT", [2, 1024, 2048], F32, kind="ExternalOutput").ap()
    wsc = nc.dram_tensor("wsc", [NPIECE, 128, PW], BF16).ap()
    dbg_d = None
    if debug:
        dbg_d = nc.dram_tensor("dbg", [128, 4096], F32, kind="ExternalOutput").ap()

    S = Sched()
    st = ExitStack()
    with st:
        def sb(name, shape, dt):
            return st.enter_context(nc.sbuf_tensor(name, shape, dt))

        vec = sb("vec_sb", [128, 64], F32)
        nbg = sb("nbg", [128, 16], F32)
        ones_bf = sb("ones_bf", [128, 128], BF16)
        bd_bf = sb("bd_bf", [128, 128], BF16)
        olo_bf = sb("olo_bf", [128, 128], BF16)
        ohi_bf = sb("ohi_bf", [128, 128], BF16)
        zero_bf = sb("zero_bf", [128, 128], BF16)
        zero_rhs = sb("zero_rhs", [128, 512], BF16)
        band = sb("band_sb", [128, 12, 144], BF16)
        poolw = sb("poolw_sb", [128, 4, 128], BF16)
        cs1 = sb("cs1", [128, 2, 512], F32)
        cs2 = sb("cs2", [128, 2, 512], F32)
        kT2 = sb("kT2", [128, 2, 2048], BF16)
        Vpad = sb("Vpad", [128, 16, 2, 192], BF16)
        utok = sb("utok", [128, 16, 512], BF16)
        xkT = sb("xkT", [128, 8, 256], BF16)
        xv = sb("xv", [128, 2, 1024], BF16)
        hbuf = [sb("hA", [128, 8, 512], F32), sb("hB", [128, 8, 512], F32)]
        n_bf = sb("n_bf", [128, 8, 512], BF16)
        U = sb("U", [128, 24, 512], BF16)
        PT = sb("PT", [128, 3, 2, 512], BF16)
        NF = 8
        Fp = sb("Fp", [128, NF, 512], F32)
        SQ = sb("SQ", [128, 3, 512], BF16)
        rfin = sb("rfin", [128, 512], F32)
        pmT = sb("pmT", [128, 4, 512], BF16)
        wbuf = sb("wbuf", [128, 3, PW], BF16)
        ps = st.enter_context(nc.psum_tensor("ps", [128, 8, 512], F32))

        def ck(name):
            CK_LOG.append((name, sum(1 for o in S.ops if o["eng"] == "pe")))
            if stop_at is not None and name == stop_at:
                raise _Stop()

        class Rot:
            def __init__(self, n):
                self.n, self.i = n, 0

            def next(self):
                k = self.i % self.n
                self.i += 1
                return k

        frot, sqrot, ptrot = Rot(NF), Rot(3), Rot(3)
        bank_i = [0]

        def bank1():
            b = bank_i[0] % 8
            bank_i[0] += 1
            return b

        def bank2():
            if bank_i[0] % 2:
                bank_i[0] += 1
            b = bank_i[0] % 8
            bank_i[0] += 2
            return b

        def PS(b):
            return ("ps", b)

        def Fs():
            k = frot.next()
            return Fp[:, k, :], ("F", k)

        def SQs():
            k = sqrot.next()
            return SQ[:, k, :], ("SQ", k)

        def Us(k):
            return U[:, k, :], ("U", k)

        def mm(out, lhsT, rhs, start, stop, reads, writes):
            S.op("pe", lambda e: e.matmul(out, lhsT=lhsT, rhs=rhs, start=start, stop=stop), reads, writes)

        def act(out, in_, func, reads, writes, bias=None, scale=None):
            kw = {}
            if bias is not None:
                kw["bias"] = bias
            if scale is not None:
                kw["scale"] = scale
            S.op("act", lambda e: e.activation(out=out, in_=in_, func=func, **kw), reads, writes)

        def tt(eng, out, in0, in1, op, reads, writes):
            S.op(eng, lambda e: e.tensor_tensor(out=out, in0=in0, in1=in1, op=op), reads, writes)

        def stt(eng, out, in0, scalar, in1, op0, op1, reads, writes):
            S.op(eng, lambda e: e.scalar_tensor_tensor(out=out, in0=in0, scalar=scalar, in1=in1, op0=op0, op1=op1),
                 reads, writes)

        def ts(eng, out, in0, s1, op0, reads, writes, s2=None, op1=None):
            if op1 is None:
                S.op(eng, lambda e: e.tensor_scalar(out=out, in0=in0, scalar1=s1, scalar2=None, op0=op0), reads, writes)
            else:
                S.op(eng, lambda e: e.tensor_scalar(out=out, in0=in0, scalar1=s1, scalar2=s2, op0=op0, op1=op1),
                     reads, writes)

        def copy(eng, out, in_, reads, writes):
            if eng == "act":
                S.op("act", lambda e: e.activation(out=out, in_=in_, func=AF.Copy), reads, writes)
            else:
                S.op(eng, lambda e: e.tensor_copy(out=out, in_=in_), reads, writes)

        def recip(out, in_, reads, writes):
            S.op("dve", lambda e: e.reciprocal(out=out, in_=in_), reads, writes)

        dbg_col = [0]

        def dump(tag, ap, key, width):
            if debug and tag in debug:
                c0 = dbg_col[0]
                dbg_col[0] += width
                src = ap
                if ap.dtype != F32:
                    t, tk = Fs()
                    copy("dve", t[:, 0:width], ap, [key], [tk])
                    src, key2 = t[:, 0:width], tk
                else:
                    key2 = key
                S.dma("sp", lambda e: e.dma_start(out=dbg_d[:, c0:c0 + width], in_=src), "dbg", reads=[key2], writes=["dbgd"])

        piece_seq = []
        for s in range(n_seq):
            piece_seq += [PI_W1, PI_W1B, PI_XKV0, PI_XKV1]
            for tb in range(4):
                piece_seq += [PI_BLK0 + i for i in range(16)]
        pstate = dict(loaded=0, cur=-1)

        seen_blk = set()

        def next_piece(expect, ahead=2):
            pstate["cur"] += 1
            k = pstate["cur"]
            assert piece_seq[k] == expect, (k, piece_seq[k], expect)
            while pstate["loaded"] < min(k + 1 + ahead, len(piece_seq)):
                i = pstate["loaded"]
                slot = i % 3
                pid = piece_seq[i]
                if pid not in seen_blk:
                    S.dma("pool", lambda e, slot=slot, pid=pid: e.dma_start(out=wbuf[:, slot, :], in_=wst[pid]),
                          ("wc", slot), writes=[("wb", slot)])
                    seen_blk.add(pid)
                    if pid >= PI_BLK0:
                        S.dma("sp", lambda e, slot=slot, pid=pid: e.dma_start(out=wsc[pid], in_=wbuf[:, slot, :]),
                              ("wbk", slot), reads=[("wb", slot)], writes=[("wsc", pid)])
                else:
                    S.dma("sp", lambda e, slot=slot, pid=pid: e.dma_start(out=wbuf[:, slot, :], in_=wsc[pid]),
                          ("w", slot), reads=[("wsc", pid)], writes=[("wb", slot)])
                pstate["loaded"] += 1
            slot = k % 3
            return wbuf[:, slot, :], ("wb", slot)

        S.dma("sp", lambda e: e.dma_start(out=vec[:], in_=vec_d), "const", writes=["vec"])
        S.dma("pool", lambda e: e.dma_start(out=band[:], in_=band_d.rearrange("p (a n) -> p a n", n=144)), "constc",
              writes=["band"])
        S.dma("pool", lambda e: e.dma_start(out=poolw[:], in_=poolw_d.rearrange("p (g d) -> p g d", d=128)), "constc",
              writes=["poolw"])
        S.op("dve", lambda e: e.memset(ones_bf[:], 1.0), writes=["ones"])
        S.op("dve", lambda e: e.memset(zero_bf[:], 0.0), writes=["zero"])
        S.op("dve", lambda e: e.memset(zero_rhs[:], 0.0), writes=["zero"])
        S.op("dve", lambda e: e.memset(bd_bf[:], 0.0), writes=["bd"])
        S.op("dve", lambda e: e.memset(bd_bf[0:64, 0:64], 1.0), writes=["bd"])
        S.op("dve", lambda e: e.memset(bd_bf[64:128, 64:128], 1.0), writes=["bd"])
        S.op("dve", lambda e: e.memset(olo_bf[:], 0.0), writes=["olo"])
        S.op("dve", lambda e: e.memset(olo_bf[:, 0:64], 1.0), writes=["olo"])
        S.op("dve", lambda e: e.memset(ohi_bf[:], 0.0), writes=["ohi"])
        S.op("dve", lambda e: e.memset(ohi_bf[:, 64:128], 1.0), writes=["ohi"])
        ts("dve", nbg[:], vec[:, C_BG:C_BG + 16], -1.0, ALU.mult, ["vec"], ["nbg"])

        hcount = [0]
        sp_i = [0]
        out_keys = []

        def load_x(s, t0):
            hi = hcount[0] % 2
            hcount[0] += 1
            h = hbuf[hi]
            hk = [("h", hi, kc) for kc in range(8)]
            src = xT[s].rearrange("(kc p) t -> p kc t", p=128)[:, :, t0:t0 + 512]
            S.dma("sp", lambda e: e.dma_start(out=h[:], in_=src), ("x", hi), writes=hk)
            return h, hk

        def load_rope(cs, key, t0):
            S.dma("sp", lambda e: e.dma_start(out=cs[:], in_=rope_d.rearrange("a p t -> p a t")[:, :, t0:t0 + 512]),
                  key, writes=[key])

        def rstd_from(ss_bank, n, width=512, dest=None, expo=-0.5):
            lt, lk = Fs()
            act(lt[:, 0:width], ps[:, ss_bank, 0:width], AF.Ln, [PS(ss_bank)], [lk], bias=EPS, scale=1.0 / n)
            rt, rk = Fs() if dest is None else dest
            act(rt[:, 0:width], lt[:, 0:width], AF.Exp, [lk], [rk], scale=expo)
            return rt, rk

        def rmsnorm(h, hk, gcol, out_fn, width=512, nchunk=8):
            rr = rms_stats(h, hk, width, nchunk)
            rms_apply(h, hk, gcol, out_fn, rr, width, nchunk)

        def rms_stats(h, hk, width=512, nchunk=8, dest=None, expo=-0.5):
            b = bank1()
            for kc in range(nchunk):
                sq, sk = SQs()
                if kc % 2 == 0:
                    tt("pool", sq[:, 0:width], h[:, kc, 0:width], h[:, kc, 0:width], ALU.mult, [hk[kc]], [sk])
                else:
                    act(sq[:, 0:width], h[:, kc, 0:width], AF.Square, [hk[kc]], [sk])
                mm(ps[:, b, 0:width], ones_bf[:], sq[:, 0:width], kc == 0, kc == nchunk - 1, [sk, "ones"], [PS(b)])
            return rstd_from(b, 1024.0, width, dest, expo)

        def scale_g(h, hk, gcol):
            for kc in range(8):
                if kc % 2 == 0:
                    act(n_bf[:, kc, :], h[:, kc, :], AF.Copy, [hk[kc], "vec"], [("n", kc)], scale=vec[:, gcol + kc:gcol + kc + 1])
                else:
                    ts("dve", n_bf[:, kc, :], h[:, kc, :], vec[:, gcol + kc:gcol + kc + 1], ALU.mult, [hk[kc], "vec"],
                       [("n", kc)])

        def rms_apply(h, hk, gcol, out_fn, rr, width=512, nchunk=8):
            rt, rk = rr
            for kc in range(nchunk):
                o, ok = out_fn(kc)
                stt("dve", o, h[:, kc, 0:width], vec[:, gcol + kc:gcol + kc + 1], rt[:, 0:width], ALU.mult, ALU.mult,
                    [hk[kc], rk, "vec"], [ok])

        def nbf_out(kc):
            return n_bf[:, kc, :], ("n", kc)

        NKEYS = [("n", kc) for kc in range(8)]

        def qk_chunk(wp, wk, off_a, off_s, gcol, gscol, cs, cskey, dst, dstkey, nfn=None):
            ba, bs_ = bank1(), bank1()
            if nfn is None:
                nfn = nbf_out
            for kc in range(8):
                na, nk = nfn(kc)
                mm(ps[:, ba, :], wp[:, off_a + kc * 128: off_a + (kc + 1) * 128], na, kc == 0, kc == 7,
                   [wk, nk], [PS(ba)])
            for kc in range(8):
                na, nk = nfn(kc)
                mm(ps[:, bs_, :], wp[:, off_s + kc * 128: off_s + (kc + 1) * 128], na, kc == 0, kc == 7,
                   [wk, nk], [PS(bs_)])
            ck("qk_mm")
            sq, sk = SQs()
            act(sq, ps[:, ba, :], AF.Square, [PS(ba)], [sk])
            ck("qk_sq")
            bss = bank1()
            mm(ps[:, bss, :], bd_bf[:], sq, True, True, [sk, "bd"], [PS(bss)])
            rt, rk = rstd_from(bss, 64.0)
            ck("qk_rstd")
            t1, k1 = Fs()
            stt("dve", t1, ps[:, ba, :], vec[:, gcol:gcol + 1], cs[:, 0, :], ALU.mult, ALU.mult, [PS(ba), "vec", cskey], [k1])
            ck("qk_t1")
            t2, k2 = Fs()
            stt("dve", t2, ps[:, bs_, :], vec[:, gscol:gscol + 1], cs[:, 1, :], ALU.mult, ALU.mult, [PS(bs_), "vec", cskey], [k2])
            tt("pool", t1, t1, t2, ALU.add, [k1, k2], [k1])
            ck("qk_t3")
            tt("pool", dst, t1, rt, ALU.mult, [k1, rk], [dstkey])

        p1_early = set()

        def p1_n(tb):
            if tb % 2 == 0:
                return (lambda kc: (n_bf[:, kc, :], ("n", kc)))
            return (lambda kc: (U[:, kc, :], ("U", kc)))

        def p1_front(s, tb):
            h, hk = load_x(s, tb * 512)
            load_rope(cs1, "cs1", tb * 512)
            rmsnorm(h, hk, C_GMIX, p1_n(tb))

        def front_a(s, tb, pre=None):
            t0 = tb * 512
            h, hk = pre if pre is not None else load_x(s, t0)
            load_rope(cs2, "cs2", t0)
            rmsnorm(h, hk, C_GMIX, nbf_out)
            return dict(h=h, hk=hk, t0=t0, s=s, tb=tb)

        def front_b(ctx):
            wp, wk = next_piece(PI_BLK0 + 0)
            for c in range(4):
                qd, qk_ = Us(c)
                qk_chunk(wp, wk, c * 1024, (4 + c) * 1024, C_GQ, C_GQS, cs2, "cs2", qd, qk_)
            ck("q")

        def pool_a(tb):
            t0 = tb * 512
            for g in range(4):
                b = bank1()
                mm(ps[:, b, :], zero_bf[:], zero_rhs[:], True, False, ["zero"], [PS(b)])
                tts = [x for x in range(tb * 4 - 1, tb * 4 + 5) if 0 <= x < 16]
                for i, ttk in enumerate(tts):
                    n0 = 128 * ttk - 8 - t0
                    lo, hi_ = max(n0, 0), min(n0 + 144, 512)
                    var = 0 if ttk == 0 else (2 if ttk == 15 else 1)
                    mm(ps[:, b, lo:hi_], utok[:, ttk, g * 128:(g + 1) * 128], band[:, var * 4 + g, lo - n0:hi_ - n0],
                       False, i == len(tts) - 1, [("ut", ttk), "band"], [PS(b)])
                pd, pk = Us(8 + g)
                copy("act", pd, ps[:, b, :], [PS(b)], [pk])

        def pool_b():
            for g in range(4):
                pd, pk = Us(8 + g)
                b2 = bank1()
                mm(ps[:, b2, :], poolw[:, g, :], pd, True, True, ["poolw", pk], [PS(b2)])
                ts("dve", pmT[:, g, :], ps[:, b2, :], vec[:, C_PS + g:C_PS + g + 1], ALU.mult, [PS(b2), "vec"], [("pm", g)])

        def attention():
            items = [(c, kt) for c in range(4) for kt in range(16)]
            r_of = {}

            def qk(i):
                c, kt = items[i]
                kvh = c // 2
                qd, qk_ = Us(c)
                bS = 2 * (i % 2)
                for e_ in range(2):
                    mm(ps[:, bS + e_, :], kT2[e_ * 64:(e_ + 1) * 64, kvh, kt * 128:(kt + 1) * 128],
                       qd[e_ * 64:(e_ + 1) * 64, :], True, True, [("kT", kvh, kt // 4), qk_], [PS(bS + e_)])

            def ex(i):
                bS = 2 * (i % 2)
                r = ptrot.next()
                r_of[i] = r
                act(PT[:, r, :, :], ps[:, bS:bS + 2, :], AF.Exp, [PS(bS), PS(bS + 1)], [("PT", r)], scale=0.125)

            def pv(i):
                c, kt = items[i]
                kvh = c // 2
                r = r_of[i]
                bE, bO = (4, 5) if c % 2 == 0 else (6, 7)
                mm(ps[:, bE, :], Vpad[:, kt, kvh, 64:192], PT[:, r, 0, :], kt == 0, kt == 15,
                   [("V", kt), "Vpad", ("PT", r)], [PS(bE)])
                mm(ps[:, bO, :], Vpad[:, kt, kvh, 0:128], PT[:, r, 1, :], kt == 0, kt == 15,
                   [("V", kt), "Vpad", ("PT", r)], [PS(bO)])
                if kt == 15:
                    rc, rck = Fs()
                    recip(rc[64:128, :], ps[64:128, bE, :], [PS(bE)], [rck])
                    recip(rc[0:64, :], ps[0:64, bO, :], [PS(bO)], [rck])
                    od, ok_ = Us(4 + c)
                    tt("dve", od[0:64, :], ps[0:64, bE, :], rc[64:128, :], ALU.mult, [PS(bE), rck], [ok_])
                    tt("dve", od[64:128, :], ps[64:128, bO, :], rc[0:64, :], ALU.mult, [PS(bO), rck], [ok_])

            qk(0)
            for i in range(len(items)):
                ex(i)
                if i + 1 < len(items):
                    qk(i + 1)
                pv(i)
            bank_i[0] = 0

        def back(s, tb, ctx):
            h, hk, t0 = ctx["h"], ctx["hk"], ctx["t0"]
            dbg0 = debug and tb == 0 and s == 0
            if dbg0:
                dump("qT", U[:, 0, :], ("U", 0), 512)
            pre = load_x(s, (tb + 1) * 512) if tb < 3 else None
            if s == 0 and tb == 1 and n_seq > 1:
                for pid in (PI_W1, PI_W1B, PI_XKV0, PI_XKV1):
                    S.dma("pool", lambda e, pid=pid: e.dma_start(out=wsc[pid], in_=wst[pid]), ("cast", pid),
                          writes=[("wsc", pid)])
            attention()
            if dbg0:
                dump("oT", U[:, 4, :], ("U", 4), 512)
            ck("attn")
            if tb == 0:
                pool_a(tb)
                pool_b()
            if dbg0:
                dump("pm", pmT[:, 0, :], ("pm", 0), 512)
            ck("pool")
            for jp in range(4):
                wp, wk = next_piece(PI_BLK0 + 1 + jp)
                gslots = []
                for jj in range(2):
                    j = jp * 2 + jj
                    base = jj * 24 * 128
                    bga, bgp = bank1(), bank1()
                    for kc in range(8):
                        mm(ps[:, bga, :], wp[:, base + kc * 128: base + (kc + 1) * 128], n_bf[:, kc, :], kc == 0, kc == 7,
                           [wk, ("n", kc)], [PS(bga)])
                    for kc in range(8):
                        mm(ps[:, bgp, :], wp[:, base + (8 + kc) * 128: base + (9 + kc) * 128], n_bf[:, kc, :], kc == 0,
                           kc == 7, [wk, ("n", kc)], [PS(bgp)])
                    ea, eak = Fs()
                    act(ea, ps[:, bga, :], AF.Exp, [PS(bga), "nbg"], [eak], bias=nbg[:, j:j + 1], scale=-1.0)
                    ep, epk = Fs()
                    act(ep, ps[:, bgp, :], AF.Exp, [PS(bgp), "nbg"], [epk], bias=nbg[:, 8 + j:9 + j], scale=-1.0)
                    act(ea, ea, AF.Ln, [eak], [eak], bias=1.0, scale=1.0)
                    act(ep, ep, AF.Ln, [epk], [epk], bias=1.0, scale=1.0)
                    act(ea, ea, AF.Exp, [eak], [eak], scale=-1.0)
                    act(ep, ep, AF.Exp, [epk], [epk], scale=-1.0)
                    gslots.append((ea, eak, ep, epk))
                for jj in range(2):
                    j = jp * 2 + jj
                    base = jj * 24 * 128
                    ea, eak, ep, epk = gslots[jj]
                    ba, bp = bank1(), bank1()
                    for kc in range(4):
                        mm(ps[:, ba, :], wp[:, base + (16 + kc) * 128: base + (17 + kc) * 128], U[:, 4 + kc, :], kc == 0,
                           kc == 3, [wk, ("U", 4 + kc)], [PS(ba)])
                    for kc in range(4):
                        mm(ps[:, bp, :], wp[:, base + (20 + kc) * 128: base + (21 + kc) * 128], pmT[:, kc, :], kc == 0,
                           kc == 3, [wk, ("pm", kc)], [PS(bp)])
                    tt("dve", ea, ps[:, ba, :], ea, ALU.mult, [PS(ba), eak], [eak])
                    tt("dve", ep, ps[:, bp, :], ep, ALU.mult, [PS(bp), epk], [epk])
                    xd, xk_ = Us(16 + j)
                    tt("pool", xd, ea, ep, ALU.add, [eak, epk], [xk_])
            wp, wk = next_piece(PI_BLK0 + 5)
            for j in range(8):
                b = bank1()
                for kc in range(8):
                    mm(ps[:, b, :], wp[:, (j * 8 + kc) * 128:(j * 8 + kc + 1) * 128], U[:, 16 + kc, :], kc == 0, kc == 7,
                       [wk, ("U", 16 + kc)], [PS(b)])
                tt("dve", h[:, j, :], ps[:, b, :], h[:, j, :], ALU.add, [PS(b), hk[j]], [hk[j]])
            if dbg0:
                dump("h1", h[:, 0, :], hk[0], 512)
            ck("h1")
            scale_g(h, hk, C_GCROSS)
            r2t, r2k = rms_stats(h, hk)
            if tb < 3:
                pool_a(tb + 1)
            wp, wk = next_piece(PI_BLK0 + 6)
            for j in range(8):
                b = bank1()
                for kc in range(8):
                    mm(ps[:, b, :], wp[:, (j * 8 + kc) * 128:(j * 8 + kc + 1) * 128], n_bf[:, kc, :], kc == 0, kc == 7,
                       [wk, ("n", kc)], [PS(b)])
                xd, xk_ = Us(j)
                tt("dve", xd, ps[:, b, :], r2t, ALU.mult, [PS(b), r2k], [xk_])
            if tb < 3:
                pool_b()
            xs = {}

            def x_s(xh):
                bS = bank2()
                xs[xh] = bS
                for mt in range(2):
                    for dc in range(2):
                        mm(ps[:, bS + mt, :], xkT[:, 2 * xh + dc, mt * 128:(mt + 1) * 128], U[:, 2 * xh + dc, :],
                           dc == 0, dc == 1, [("xk", 2 * xh + dc), ("U", 2 * xh + dc)], [PS(bS + mt)])

            def x_e(xh):
                bS = xs[xh]
                r = ptrot.next()
                xs[("r", xh)] = r
                act(PT[:, r, :, :], ps[:, bS:bS + 2, :], AF.Exp, [PS(bS), PS(bS + 1)], [("PT", r)], scale=1.0 / 16.0)

            def x_pv(xh):
                r = xs[("r", xh)]
                bsum = bank1()
                for mt in range(2):
                    mm(ps[:, bsum, :], ones_bf[:], PT[:, r, mt, :], mt == 0, mt == 1, ["ones", ("PT", r)], [PS(bsum)])
                rc, rck = Fs()
                act(rc, ps[:, bsum, :], AF.Ln, [PS(bsum)], [rck])
                act(rc, rc, AF.Exp, [rck], [rck], scale=-1.0)
                for dc in range(2):
                    b = bank1()
                    for mt in range(2):
                        mm(ps[:, b, :], xv[:, mt, (2 * xh + dc) * 128:(2 * xh + dc + 1) * 128], PT[:, r, mt, :],
                           mt == 0, mt == 1, [("xv", mt), ("PT", r)], [PS(b)])
                    od, ok_ = Us(8 + 2 * xh + dc)
                    tt("dve", od, ps[:, b, :], rc, ALU.mult, [PS(b), rck], [ok_])

            x_s(0)
            for xh in range(4):
                x_e(xh)
                if xh + 1 < 4:
                    x_s(xh + 1)
                x_pv(xh)
            wp, wk = next_piece(PI_BLK0 + 7)
            for j in range(8):
                b = bank1()
                for kc in range(8):
                    mm(ps[:, b, :], wp[:, (j * 8 + kc) * 128:(j * 8 + kc + 1) * 128], U[:, 8 + kc, :], kc == 0, kc == 7,
                       [wk, ("U", 8 + kc)], [PS(b)])
                tt("dve", h[:, j, :], ps[:, b, :], h[:, j, :], ALU.add, [PS(b), hk[j]], [hk[j]])
            if dbg0:
                dump("h2", h[:, 0, :], hk[0], 512)
            ck("h2")
            scale_g(h, hk, C_GFFN)
            r3t, r3k = rms_stats(h, hk, dest=(rfin[:], "rfin"), expo=-1.0)
            nctx = None
            for p_ in range(2):
                for half in range(2):
                    wp, wk = next_piece(PI_BLK0 + 8 + p_ * 4 + half)
                    for i8 in range(8):
                        i = half * 8 + i8
                        b = bank1()
                        for kc in range(8):
                            mm(ps[:, b, :], wp[:, (i8 * 8 + kc) * 128:(i8 * 8 + kc + 1) * 128], n_bf[:, kc, :], kc == 0,
                               kc == 7, [wk, ("n", kc)], [PS(b)])
                        rl, rlk = Fs()
                        if i % 2 == 0:
                            act(rl, ps[:, b, :], AF.Relu, [PS(b)], [rlk])
                        else:
                            ts("dve", rl, ps[:, b, :], 0.0, ALU.max, [PS(b)], [rlk])
                        slot = (p_ * 16 + i) % 24
                        hd, hdk = Us(slot)
                        tt("pool" if i % 2 == 0 else "dve", hd, rl, rl, ALU.mult, [rlk], [hdk])
                if p_ == 1 and tb < 3:
                    nctx = front_a(s, tb + 1, pre)
                elif p_ == 1 and tb == 3 and s + 1 < n_seq:
                    p1_front(s + 1, 0)
                    p1_early.add(s + 1)
                for jh in range(2):
                    wp, wk = next_piece(PI_BLK0 + 10 + p_ * 4 + jh)
                    for j4 in range(4):
                        j = jh * 4 + j4
                        b = bank1()
                        for kc in range(16):
                            slot = (p_ * 16 + kc) % 24
                            mm(ps[:, b, :], wp[:, (j4 * 16 + kc) * 128:(j4 * 16 + kc + 1) * 128], U[:, slot, :], kc == 0,
                               kc == 15, [wk, ("U", slot)], [PS(b)])
                        ft, fk = Fs()
                        tt("dve", ft, ps[:, b, :], r3t, ALU.mult, [PS(b), r3k], [fk])
                        tt("dve", h[:, j, :], h[:, j, :], ft, ALU.add, [hk[j], fk], [hk[j]])
            ck("ffn")
            rr = rms_stats(h, hk, dest=(rfin[:], "rfin"))
            if nctx is not None:
                front_b(nctx)
            rms_apply(h, hk, C_GFINAL, lambda kc: (h[:, kc, :], hk[kc]), rr)
            ck("fnorm")
            dst = outT[s].rearrange("(kc p) t -> p kc t", p=128)[:, :, t0:t0 + 512]
            S.dma("sp", lambda e: e.dma_start(out=dst, in_=h[:]), ("o", hk[0][1]), reads=hk, writes=[("outT", s, tb)])
            out_keys.append(("outT", s, tb))
            ck("blk")
            return nctx

        try:
          for s in range(n_seq):
              wp, wk = next_piece(PI_W1, ahead=1)
              wpb, wkb = next_piece(PI_W1B, ahead=1)
              if s == 0:
                  S.op("pool", lambda e: e.memset(Vpad[:], 1.0), writes=["Vpad"])
              mnv = [U[:, 8 + kc // 2, (kc % 2) * 256:(kc % 2) * 256 + 256] for kc in range(8)]
              mnk = [("U", 8 + kc // 2) for kc in range(8)]
              memh = {}

              def k_stage(tb):
                  nfn = p1_n(tb)
                  for kvh in range(2):
                      qk_chunk(wp, wk, kvh * 2048, kvh * 2048 + 1024, C_GK, C_GKS, cs1, "cs1",
                               kT2[:, kvh, tb * 512:tb * 512 + 512], ("kT", kvh, tb), nfn)

              def v_stage(tb):
                  nfn = p1_n(tb)
                  for tt_ in range(4):
                      kt = tb * 4 + tt_
                      bv = bank1()
                      for kc in range(8):
                          na, nk = nfn(kc)
                          mm(ps[:, bv, 0:128], na[:, tt_ * 128:(tt_ + 1) * 128],
                             wp[:, 4096 + kc * 128: 4096 + (kc + 1) * 128], kc == 0, kc == 7, [wk, nk], [PS(bv)])
                      copy("act", Vpad[:, kt, :, 64:128], ps[:, bv, 0:128].rearrange("p (a d) -> p a d", d=64),
                           [PS(bv), "Vpad"], [("V", kt)])

              def u_stage(tb):
                  nfn = p1_n(tb)
                  for tt_ in range(4):
                      kt = tb * 4 + tt_
                      bu = bank1()
                      for kc in range(8):
                          na, nk = nfn(kc)
                          mm(ps[:, bu, :], na[:, tt_ * 128:(tt_ + 1) * 128], wpb[:, kc * 512:(kc + 1) * 512],
                             kc == 0, kc == 7, [wkb, nk], [PS(bu)])
                      copy("dve", utok[:, kt, :], ps[:, bu, :], [PS(bu)], [("ut", kt)])

              def mem_load():
                  hi = hcount[0] % 2
                  hcount[0] += 1
                  hm = hbuf[hi]
                  hmk = [("h", hi, kc) for kc in range(8)]
                  S.dma("sp", lambda e, hm=hm, s=s: e.dma_start(out=hm[:, :, 0:256],
                                                                  in_=memT[s].rearrange("(kc p) t -> p kc t", p=128)),
                        ("x", hi), writes=hmk)
                  memh["h"], memh["k"] = hm, hmk

              def mem_norm():
                  rmsnorm(memh["h"], memh["k"], C_GMEM, lambda kc: (mnv[kc], mnk[kc]), width=256)

              def xk_stage(wx, wxk):
                  for j in range(8):
                      b = bank1()
                      for kc in range(8):
                          mm(ps[:, b, 0:256], wx[:, (j * 8 + kc) * 128:(j * 8 + kc + 1) * 128], mnv[kc], kc == 0, kc == 7,
                             [wxk, mnk[kc]], [PS(b)])
                      copy("act" if j % 2 else "dve", xkT[:, j, :], ps[:, b, 0:256], [PS(b)], [("xk", j)])

              def xv_stage(wx, wxk):
                  for mt in range(2):
                      for jq in range(2):
                          b = bank1()
                          for jj in range(4):
                              j = jq * 4 + jj
                              for kc in range(8):
                                  mm(ps[:, b, jj * 128:(jj + 1) * 128], mnv[kc][:, mt * 128:(mt + 1) * 128],
                                     wx[:, (j * 8 + kc) * 128:(j * 8 + kc + 1) * 128], kc == 0, kc == 7, [wxk, mnk[kc]],
                                     [PS(b)])
                          copy("act" if jq % 2 else "dve", xv[:, mt, jq * 512:(jq + 1) * 512], ps[:, b, :], [PS(b)],
                               [("xv", mt)])

              if s not in p1_early:
                  p1_front(s, 0)
              for tb in range(3):
                  ck("p1_norm")
                  k_stage(tb)
                  ck("p1_k")
                  p1_front(s, tb + 1)
                  if tb == 2:
                      mem_load()
                  v_stage(tb)
                  u_stage(tb)
                  ck("p1_blk")
              k_stage(3)
              v_stage(3)
              wx0, wx0k = next_piece(PI_XKV0, ahead=1)
              mem_norm()
              u_stage(3)
              if debug:
                  dump("kT", kT2[:, 0, 0:512], ("kT", 0, 0), 512)
                  dump("V", Vpad[:, 0, 0, 64:128], ("V", 0), 64)
                  dump("ut", utok[:, 0, :], ("ut", 0), 512)
              ck("p1")
              xk_stage(wx0, wx0k)
              wx1, wx1k = next_piece(PI_XKV1)
              ctx = front_a(s, 0)
              xv_stage(wx1, wx1k)
              if debug:
                  dump("xk", xkT[:, 0, :], ("xk", 0), 256)
                  dump("xv", xv[:, 0, 0:512], ("xv", 0), 512)
              ck("mem")
              front_b(ctx)
              for tb in range(4):
                  ctx = back(s, tb, ctx)
        except _Stop:
            pass
        S.op("sp", lambda e: e.nop(), reads=out_keys + ["dbgd"])
        S.emit(nc, st)
    return nc


def _cm(w):
    K, N = w.shape
    return w.reshape(K // 128, 128, N // 128, 128).transpose(2, 1, 0, 3)


def _perm64():
    perm = np.zeros(64, np.int64)
    perm_sw = np.zeros(64, np.int64)
    for half in range(2):
        for axis in range(2):
            for f in range(16):
                ip = half * 32 + axis * 16 + f
                perm[ip] = axis * 32 + half * 16 + f
                perm_sw[ip] = axis * 32 + (1 - half) * 16 + f
    return perm, perm_sw


def _pack_weights(w_in, w_attn_up, w_pool_up, w_out, w_xq, w_xkv, w_xo, w_ff1, w_ff2):
    perm, perm_sw = _perm64()
    wst = np.zeros((NPIECE, 128, PW), np.float32)

    def flat(a):
        return np.ascontiguousarray(a).reshape(128, -1)

    def put(pid, off, a):
        a = flat(a)
        wst[pid, :, off:off + a.shape[1]] = a
        return off + a.shape[1]

    off = 0
    for kvh in range(2):
        cols = 512 + kvh * 64 + perm
        cols_sw = 512 + kvh * 64 + perm_sw
        for cc in (cols, cols_sw):
            wk = w_in[:, np.concatenate([cc, cc])]
            off = put(PI_W1, off, _cm(wk)[0])
    off = put(PI_W1, off, _cm(w_in[:, 640:768])[0])
    put(PI_W1B, 0, w_in[:, 768:1280].reshape(8, 128, 512).transpose(1, 0, 2))
    cx = _cm(w_xkv)
    put(PI_XKV0, 0, cx[0:8].transpose(1, 0, 2, 3))
    put(PI_XKV1, 0, cx[8:16].transpose(1, 0, 2, 3))
    qcols = np.concatenate([h * 64 + perm for h in range(8)])
    qscols = np.concatenate([h * 64 + perm_sw for h in range(8)])
    off = put(PI_BLK0 + 0, 0, _cm(w_in[:, qcols]).transpose(1, 0, 2, 3))
    put(PI_BLK0 + 0, off, _cm(w_in[:, qscols]).transpose(1, 0, 2, 3))
    cga = _cm(w_in[:, 1280:2304])
    cgp = _cm(w_in[:, 2304:3328])
    cau = _cm(w_attn_up)
    cpu = _cm(w_pool_up)
    for jp in range(4):
        off = 0
        for jj in range(2):
            j = jp * 2 + jj
            off = put(PI_BLK0 + 1 + jp, off, cga[j])
            off = put(PI_BLK0 + 1 + jp, off, cgp[j])
            off = put(PI_BLK0 + 1 + jp, off, cau[j])
            off = put(PI_BLK0 + 1 + jp, off, cpu[j])
    put(PI_BLK0 + 5, 0, _cm(w_out).transpose(1, 0, 2, 3))
    put(PI_BLK0 + 6, 0, _cm(w_xq).transpose(1, 0, 2, 3))
    put(PI_BLK0 + 7, 0, _cm(w_xo).transpose(1, 0, 2, 3))
    c1 = _cm(w_ff1)
    c2 = _cm(w_ff2)
    for p_ in range(2):
        for half in range(2):
            i0 = p_ * 16 + half * 8
            put(PI_BLK0 + 8 + p_ * 4 + half, 0, c1[i0:i0 + 8].transpose(1, 0, 2, 3))
        for jh in range(2):
            put(PI_BLK0 + 10 + p_ * 4 + jh, 0, c2[jh * 4:(jh + 1) * 4, :, p_ * 16:(p_ + 1) * 16, :].transpose(1, 0, 2, 3))
    return wst


def _const_tables():
    S_, GW = 2048, 64
    t = np.arange(S_)
    posn = np.stack([t // GW, t % GW], 0).astype(np.float32)
    inv = (10000.0 ** (-np.arange(0, 32, 2, dtype=np.float32) / 32.0)).astype(np.float32)
    rope = np.zeros((2, 128, S_), np.float32)
    for hh in range(2):
        for half in range(2):
            for axis in range(2):
                for f in range(16):
                    p = hh * 64 + half * 32 + axis * 16 + f
                    ang = (posn[axis] * inv[f]).astype(np.float32)
                    rope[0, p] = np.cos(ang)
                    rope[1, p] = np.sin(ang) * (-1.0 if half == 0 else 1.0)
    band = np.zeros((128, 3, 4, 144), np.float32)
    for var, ttk in enumerate((0, 5, 15)):
        for g, w in enumerate((2, 4, 8, 16)):
            for n in range(-8, 136):
                tp = 128 * ttk + n
                if tp < 0 or tp >= S_:
                    continue
                lo = min(max(tp - w // 2, 0), S_)
                hi = min(max(tp + (w - w // 2), 0), S_)
                cnt = float(hi - lo)
                for k in range(128):
                    tk = 128 * ttk + k
                    v = 0.0
                    if lo <= tk < hi:
                        v += 1.0 / cnt
                    if tk == tp:
                        v -= 1.0
                    band[k, var, g, n + 8] = v
    return rope, band.reshape(128, 12 * 144)


def _prep(inputs):
    g = lambda k: np.asarray(inputs[k], dtype=np.float32)
    x, mem = g("x"), g("mem")
    perm, perm_sw = _perm64()
    wst = _pack_weights(g("w_in")[0], g("w_attn_up")[0], g("w_pool_up")[0], g("w_out")[0], g("w_xq")[0],
                        g("w_xkv")[0], g("w_xo")[0], g("w_ff1")[0], g("w_ff2")[0])
    vec = np.zeros((128, 64), np.float32)
    for c0, name in ((C_GMIX, "g_mix"), (C_GCROSS, "g_cross"), (C_GFFN, "g_ffn"), (C_GMEM, "g_mem")):
        vec[:, c0:c0 + 8] = g(name)[0].reshape(8, 128).T
    vec[:, C_GFINAL:C_GFINAL + 8] = g("g_final").reshape(8, 128).T
    vec[:, C_BG:C_BG + 16] = g("b_gate")[0].reshape(16, 128).T
    vec[:, C_PS:C_PS + 4] = g("pool_scale")[0].reshape(4, 128).T
    gq, gk = g("g_q")[0], g("g_k")[0]
    vec[:, C_GQ] = np.tile(gq[perm], 2)
    vec[:, C_GQS] = np.tile(gq[perm_sw], 2)
    vec[:, C_GK] = np.tile(gk[perm], 2)
    vec[:, C_GKS] = np.tile(gk[perm_sw], 2)
    poolw = np.ascontiguousarray(g("pool_w")[0].transpose(1, 0, 2)).reshape(128, 512)
    rope, band = _const_tables()
    xT = np.ascontiguousarray(x.transpose(0, 2, 1))
    memT = np.ascontiguousarray(mem.transpose(0, 2, 1))
    in_maps = []
    for c in range(8):
        in_maps.append(dict(xT=xT[2 * c:2 * c + 2], memT=memT[2 * c:2 * c + 2], wst=wst, vec=vec, rope=rope,
                            band=band, poolw=poolw))
    return in_maps


def kernel(**inputs):
    in_maps = _prep(inputs)
    nc = build_core()
    res = run_bass_kernel_spmd(nc, in_maps, core_ids=list(range(8)))
    outT = np.stack([r["outT"] for r in res.results], 0)
    out = outT.reshape(16, 1024, 2048).transpose(0, 2, 1)
    return np.ascontiguousarray(out).astype(np.float32)
```

```python
import numpy as np
from contextlib import ExitStack
import concourse.bass as bass
import concourse.mybir as mybir
from concourse.bass_utils import run_bass_kernel_spmd

F32 = mybir.dt.float32
BF16 = mybir.dt.bfloat16
ALU = mybir.AluOpType
AF = mybir.ActivationFunctionType

ENGS = ("pe", "act", "dve", "pool", "sp")
EPS = 1e-6
NPIECE = 20
PW = 8192


class Sched:
    SAME_ENG_DIST = 10 ** 9
    SEM_CAP = 20000

    def __init__(self):
        self.ops = []

    def op(self, eng, fn, reads=(), writes=()):
        isps = lambda k: isinstance(k, tuple) and k[0] == "ps"
        w = list(writes) + [k for k in reads if isps(k)]
        r = [k for k in reads if not isps(k)]
        self.ops.append(dict(eng=eng, fn=fn, r=tuple(r), w=tuple(w), dma=None))

    def dma(self, q, fn, semkey, reads=(), writes=(), after=()):
        self.ops.append(dict(eng=q, fn=fn, r=tuple(reads), w=tuple(writes), dma=semkey, after=tuple(after)))

    def analyze(self):
        last_w, readers = {}, {}
        pos = {e: 0 for e in ENGS}
        dma_cnt = {}
        for i, o in enumerate(self.ops):
            deps = set()
            for r in o["r"] + o.get("after", ()):
                if r in last_w:
                    deps.add(last_w[r])
            for w in o["w"]:
                if w in last_w:
                    deps.add(last_w[w])
                deps.update(readers.get(w, ()))
            deps.discard(i)
            o["pos"] = pos[o["eng"]]
            pos[o["eng"]] += 1
            dma_waits, eng_deps = {}, {}
            for d in deps:
                od = self.ops[d]
                if od["dma"] is not None:
                    k = od["dma"]
                    dma_waits[k] = dma_cnt[k] * 16
                else:
                    e2 = od["eng"]
                    if e2 == o["eng"]:
                        if e2 == "pe":
                            continue
                        if o["pos"] - od["pos"] > self.SAME_ENG_DIST:
                            continue
                    if e2 not in eng_deps or self.ops[eng_deps[e2]]["pos"] < od["pos"]:
                        eng_deps[e2] = d
            o["dma_waits"] = dma_waits
            o["eng_deps"] = eng_deps
            for d in eng_deps.values():
                self.ops[d]["ms"] = True
            if o["dma"] is not None:
                dma_cnt[o["dma"]] = dma_cnt.get(o["dma"], 0) + 1
            for r in o["r"]:
                readers.setdefault(r, []).append(i)
            for w in o["w"]:
                last_w[w] = i
                readers[w] = []
        cnt = {e: 0 for e in ENGS}
        for o in self.ops:
            if o.get("ms"):
                o["msidx"] = cnt[o["eng"]]
                cnt[o["eng"]] += 1
        self.ms_count = cnt
        self.dma_keys = list(dma_cnt.keys())
        self.dma_total = dict(dma_cnt)

    def emit(self, nc, stack):
        self.analyze()
        CAP = self.SEM_CAP
        esems = {}
        for e in ENGS:
            n = max(1, (self.ms_count[e] + CAP - 1) // CAP)
            esems[e] = [stack.enter_context(nc.semaphore(f"s_{e}{k}")) for k in range(n)]
        dsems = {k: stack.enter_context(nc.semaphore(f"d_{k}")) for k in self.dma_keys}
        block = stack.enter_context(nc.Block())
        ops = self.ops

        def run(ename, eng):
            waited = {}
            for o in ops:
                if o["eng"] != ename:
                    continue
                for k, v in o["dma_waits"].items():
                    if waited.get(("d", k), 0) < v:
                        eng.wait_ge(dsems[k], v)
                        waited[("d", k)] = v
                for e2, d in o["eng_deps"].items():
                    mi = ops[d]["msidx"]
                    si, v = mi // CAP, mi % CAP + 1
                    if waited.get((e2, si), 0) < v:
                        eng.wait_ge(esems[e2][si], v)
                        waited[(e2, si)] = v
                ins = o["fn"](eng)
                if o["dma"] is not None:
                    ins.then_inc(dsems[o["dma"]], 16)
                elif o.get("ms"):
                    ins.then_inc(esems[ename][o["msidx"] // CAP], 1)

        @block.tensor
        def _(eng):
            run("pe", eng)

        @block.scalar
        def _(eng):
            run("act", eng)

        @block.vector
        def _(eng):
            run("dve", eng)

        @block.gpsimd
        def _(eng):
            run("pool", eng)

        @block.sync
        def _(eng):
            run("sp", eng)
            for k in self.dma_keys:
                eng.wait_ge(dsems[k], self.dma_total[k] * 16)


C_GMIX, C_GCROSS, C_GFFN, C_GFINAL, C_GMEM, C_BG, C_PS, C_GQ, C_GQS, C_GK, C_GKS = 0, 8, 16, 24, 32, 40, 56, 60, 61, 62, 63
PI_W1, PI_W1B, PI_XKV0, PI_XKV1, PI_BLK0 = 0, 1, 2, 3, 4


CK_LOG = []


class _Stop(Exception):
    pass


def build_core(n_seq=2, debug=None, stop_at=None):
    nc = bass.Bass("TRN2", target_bir_lowering=False)
    xT = nc.dram_tensor("xT", [2, 1024, 2048], F32, kind="ExternalInput").ap()
    memT = nc.dram_tensor("memT", [2, 1024, 256], F32, kind="ExternalInput").ap()
    wst = nc.dram_tensor("wst", [NPIECE, 128, PW], F32, kind="ExternalInput").ap()
    vec_d = nc.dram_tensor("vec", [128, 64], F32, kind="ExternalInput").ap()
    rope_d = nc.dram_tensor("rope", [2, 128, 2048], F32, kind="ExternalInput").ap()
    band_d = nc.dram_tensor("band", [128, 12 * 144], F32, kind="ExternalInput").ap()
    poolw_d = nc.dram_tensor("poolw", [128, 512], F32, kind="ExternalInput").ap()
    outT = nc.dram_tensor("outT", [2, 1024, 2048], F32, kind="ExternalOutput").ap()
    wsc = nc.dram_tensor("wsc", [NPIECE, 128, PW], BF16).ap()
    dbg_d = None
    if debug:
        dbg_d = nc.dram_tensor("dbg", [128, 4096], F32, kind="ExternalOutput").ap()

    S = Sched()
    st = ExitStack()
    with st:
        def sb(name, shape, dt):
            return st.enter_context(nc.sbuf_tensor(name, shape, dt))

        vec = sb("vec_sb", [128, 64], F32)
        nbg = sb("nbg", [128, 16], F32)
        ones_bf = sb("ones_bf", [128, 128], BF16)
        bd_bf = sb("bd_bf", [128, 128], BF16)
        olo_bf = sb("olo_bf", [128, 128], BF16)
        ohi_bf = sb("ohi_bf", [128, 128], BF16)
        zero_bf = sb("zero_bf", [128, 128], BF16)
        zero_rhs = sb("zero_rhs", [128, 512], BF16)
        band = sb("band_sb", [128, 12, 144], BF16)
        poolw = sb("poolw_sb", [128, 4, 128], BF16)
        cs1 = sb("cs1", [128, 2, 512], F32)
        cs2 = sb("cs2", [128, 2, 512], F32)
        kT2 = sb("kT2", [128, 2, 2048], BF16)
        Vpad = sb("Vpad", [128, 16, 2, 192], BF16)
        utok = sb("utok", [128, 16, 512], BF16)
        xkT = sb("xkT", [128, 8, 256], BF16)
        xv = sb("xv", [128, 2, 1024], BF16)
        hbuf = [sb("hA", [128, 8, 512], F32), sb("hB", [128, 8, 512], F32)]
        n_bf = sb("n_bf", [128, 8, 512], BF16)
        U = sb("U", [128, 24, 512], BF16)
        PT = sb("PT", [128, 3, 2, 512], BF16)
        NF = 8
        Fp = sb("Fp", [128, NF, 512], F32)
        SQ = sb("SQ", [128, 3, 512], BF16)
        rfin = sb("rfin", [128, 512], F32)
        pmT = sb("pmT", [128, 4, 512], BF16)
        wbuf = sb("wbuf", [128, 3, PW], BF16)
        ps = st.enter_context(nc.psum_tensor("ps", [128, 8, 512], F32))

        def ck(name):
            CK_LOG.append((name, sum(1 for o in S.ops if o["eng"] == "pe")))
            if stop_at is not None and name == stop_at:
                raise _Stop()

        class Rot:
            def __init__(self, n):
                self.n, self.i = n, 0

            def next(self):
                k = self.i % self.n
                self.i += 1
                return k

        frot, sqrot, ptrot = Rot(NF), Rot(3), Rot(3)
        bank_i = [0]

        def bank1():
            b = bank_i[0] % 8
            bank_i[0] += 1
            return b

        def bank2():
            if bank_i[0] % 2:
                bank_i[0] += 1
            b = bank_i[0] % 8
            bank_i[0] += 2
            return b

        def PS(b):
            return ("ps", b)

        def Fs():
            k = frot.next()
            return Fp[:, k, :], ("F", k)

        def SQs():
            k = sqrot.next()
            return SQ[:, k, :], ("SQ", k)

        def Us(k):
            return U[:, k, :], ("U", k)

        def mm(out, lhsT, rhs, start, stop, reads, writes):
            S.op("pe", lambda e: e.matmul(out, lhsT=lhsT, rhs=rhs, start=start, stop=stop), reads, writes)

        def act(out, in_, func, reads, writes, bias=None, scale=None):
            kw = {}
            if bias is not None:
                kw["bias"] = bias
            if scale is not None:
                kw["scale"] = scale
            S.op("act", lambda e: e.activation(out=out, in_=in_, func=func, **kw), reads, writes)

        def tt(eng, out, in0, in1, op, reads, writes):
            S.op(eng, lambda e: e.tensor_tensor(out=out, in0=in0, in1=in1, op=op), reads, writes)

        def stt(eng, out, in0, scalar, in1, op0, op1, reads, writes):
            S.op(eng, lambda e: e.scalar_tensor_tensor(out=out, in0=in0, scalar=scalar, in1=in1, op0=op0, op1=op1),
                 reads, writes)

        def ts(eng, out, in0, s1, op0, reads, writes, s2=None, op1=None):
            if op1 is None:
                S.op(eng, lambda e: e.tensor_scalar(out=out, in0=in0, scalar1=s1, scalar2=None, op0=op0), reads, writes)
            else:
                S.op(eng, lambda e: e.tensor_scalar(out=out, in0=in0, scalar1=s1, scalar2=s2, op0=op0, op1=op1),
                     reads, writes)

        def copy(eng, out, in_, reads, writes):
            if eng == "act":
                S.op("act", lambda e: e.activation(out=out, in_=in_, func=AF.Copy), reads, writes)
            else:
                S.op(eng, lambda e: e.tensor_copy(out=out, in_=in_), reads, writes)

        def recip(out, in_, reads, writes):
            S.op("dve", lambda e: e.reciprocal(out=out, in_=in_), reads, writes)

        dbg_col = [0]

        def dump(tag, ap, key, width):
            if debug and tag in debug:
                c0 = dbg_col[0]
                dbg_col[0] += width
                src = ap
                if ap.dtype != F32:
                    t, tk = Fs()
                    copy("dve", t[:, 0:width], ap, [key], [tk])
                    src, key2 = t[:, 0:width], tk
                else:
                    key2 = key
                S.dma("sp", lambda e: e.dma_start(out=dbg_d[:, c0:c0 + width], in_=src), "dbg", reads=[key2], writes=["dbgd"])

        piece_seq = []
        for s in range(n_seq):
            piece_seq += [PI_W1, PI_W1B, PI_XKV0, PI_XKV1]
            for tb in range(4):
                piece_seq += [PI_BLK0 + i for i in range(16)]
        pstate = dict(loaded=0, cur=-1)

        seen_blk = set()

        def next_piece(expect, ahead=2):
            pstate["cur"] += 1
            k = pstate["cur"]
            assert piece_seq[k] == expect, (k, piece_seq[k], expect)
            while pstate["loaded"] < min(k + 1 + ahead, len(piece_seq)):
                i = pstate["loaded"]
                slot = i % 3
                pid = piece_seq[i]
                if pid not in seen_blk:
                    S.dma("pool", lambda e, slot=slot, pid=pid: e.dma_start(out=wbuf[:, slot, :], in_=wst[pid]),
                          ("wc", slot), writes=[("wb", slot)])
                    seen_blk.add(pid)
                    if pid >= PI_BLK0:
                        S.dma("sp", lambda e, slot=slot, pid=pid: e.dma_start(out=wsc[pid], in_=wbuf[:, slot, :]),
                              ("wbk", slot), reads=[("wb", slot)], writes=[("wsc", pid)])
                else:
                    S.dma("sp", lambda e, slot=slot, pid=pid: e.dma_start(out=wbuf[:, slot, :], in_=wsc[pid]),
                          ("w", slot), reads=[("wsc", pid)], writes=[("wb", slot)])
                pstate["loaded"] += 1
            slot = k % 3
            return wbuf[:, slot, :], ("wb", slot)

        S.dma("sp", lambda e: e.dma_start(out=vec[:], in_=vec_d), "const", writes=["vec"])
        S.dma("pool", lambda e: e.dma_start(out=band[:], in_=band_d.rearrange("p (a n) -> p a n", n=144)), "constc",
              writes=["band"])
        S.dma("pool", lambda e: e.dma_start(out=poolw[:], in_=poolw_d.rearrange("p (g d) -> p g d", d=128)), "constc",
              writes=["poolw"])
        S.op("dve", lambda e: e.memset(ones_bf[:], 1.0), writes=["ones"])
        S.op("dve", lambda e: e.memset(zero_bf[:], 0.0), writes=["zero"])
        S.op("dve", lambda e: e.memset(zero_rhs[:], 0.0), writes=["zero"])
        S.op("dve", lambda e: e.memset(bd_bf[:], 0.0), writes=["bd"])
        S.op("dve", lambda e: e.memset(bd_bf[0:64, 0:64], 1.0), writes=["bd"])
        S.op("dve", lambda e: e.memset(bd_bf[64:128, 64:128], 1.0), writes=["bd"])
        S.op("dve", lambda e: e.memset(olo_bf[:], 0.0), writes=["olo"])
        S.op("dve", lambda e: e.memset(olo_bf[:, 0:64], 1.0), writes=["olo"])
        S.op("dve", lambda e: e.memset(ohi_bf[:], 0.0), writes=["ohi"])
        S.op("dve", lambda e: e.memset(ohi_bf[:, 64:128], 1.0), writes=["ohi"])
        ts("dve", nbg[:], vec[:, C_BG:C_BG + 16], -1.0, ALU.mult, ["vec"], ["nbg"])

        hcount = [0]
        sp_i = [0]
        out_keys = []

        def load_x(s, t0):
            hi = hcount[0] % 2
            hcount[0] += 1
            h = hbuf[hi]
            hk = [("h", hi, kc) for kc in range(8)]
            src = xT[s].rearrange("(kc p) t -> p kc t", p=128)[:, :, t0:t0 + 512]
            S.dma("sp", lambda e: e.dma_start(out=h[:], in_=src), ("x", hi), writes=hk)
            return h, hk

        def load_rope(cs, key, t0):
            S.dma("sp", lambda e: e.dma_start(out=cs[:], in_=rope_d.rearrange("a p t -> p a t")[:, :, t0:t0 + 512]),
                  key, writes=[key])

        def rstd_from(ss_bank, n, width=512, dest=None, expo=-0.5):
            lt, lk = Fs()
            act(lt[:, 0:width], ps[:, ss_bank, 0:width], AF.Ln, [PS(ss_bank)], [lk], bias=EPS, scale=1.0 / n)
            rt, rk = Fs() if dest is None else dest
            act(rt[:, 0:width], lt[:, 0:width], AF.Exp, [lk], [rk], scale=expo)
            return rt, rk

        def rmsnorm(h, hk, gcol, out_fn, width=512, nchunk=8):
            rr = rms_stats(h, hk, width, nchunk)
            rms_apply(h, hk, gcol, out_fn, rr, width, nchunk)

        def rms_stats(h, hk, width=512, nchunk=8, dest=None, expo=-0.5):
            b = bank1()
            for kc in range(nchunk):
                sq, sk = SQs()
                if kc % 2 == 0:
                    tt("pool", sq[:, 0:width], h[:, kc, 0:width], h[:, kc, 0:width], ALU.mult, [hk[kc]], [sk])
                else:
                    act(sq[:, 0:width], h[:, kc, 0:width], AF.Square, [hk[kc]], [sk])
                mm(ps[:, b, 0:width], ones_bf[:], sq[:, 0:width], kc == 0, kc == nchunk - 1, [sk, "ones"], [PS(b)])
            return rstd_from(b, 1024.0, width, dest, expo)

        def scale_g(h, hk, gcol):
            for kc in range(8):
                if kc % 2 == 0:
                    act(n_bf[:, kc, :], h[:, kc, :], AF.Copy, [hk[kc], "vec"], [("n", kc)], scale=vec[:, gcol + kc:gcol + kc + 1])
                else:
                    ts("dve", n_bf[:, kc, :], h[:, kc, :], vec[:, gcol + kc:gcol + kc + 1], ALU.mult, [hk[kc], "vec"],
                       [("n", kc)])

        def rms_apply(h, hk, gcol, out_fn, rr, width=512, nchunk=8):
            rt, rk = rr
            for kc in range(nchunk):
                o, ok = out_fn(kc)
                stt("dve", o, h[:, kc, 0:width], vec[:, gcol + kc:gcol + kc + 1], rt[:, 0:width], ALU.mult, ALU.mult,
                    [hk[kc], rk, "vec"], [ok])

        def nbf_out(kc):
            return n_bf[:, kc, :], ("n", kc)

        NKEYS = [("n", kc) for kc in range(8)]

        def qk_chunk(wp, wk, off_a, off_s, gcol, gscol, cs, cskey, dst, dstkey, nfn=None):
            ba, bs_ = bank1(), bank1()
            if nfn is None:
                nfn = nbf_out
            for kc in range(8):
                na, nk = nfn(kc)
                mm(ps[:, ba, :], wp[:, off_a + kc * 128: off_a + (kc + 1) * 128], na, kc == 0, kc == 7,
                   [wk, nk], [PS(ba)])
            for kc in range(8):
                na, nk = nfn(kc)
                mm(ps[:, bs_, :], wp[:, off_s + kc * 128: off_s + (kc + 1) * 128], na, kc == 0, kc == 7,
                   [wk, nk], [PS(bs_)])
            ck("qk_mm")
            sq, sk = SQs()
            act(sq, ps[:, ba, :], AF.Square, [PS(ba)], [sk])
            ck("qk_sq")
            bss = bank1()
            mm(ps[:, bss, :], bd_bf[:], sq, True, True, [sk, "bd"], [PS(bss)])
            rt, rk = rstd_from(bss, 64.0)
            ck("qk_rstd")
            t1, k1 = Fs()
            stt("dve", t1, ps[:, ba, :], vec[:, gcol:gcol + 1], cs[:, 0, :], ALU.mult, ALU.mult, [PS(ba), "vec", cskey], [k1])
            ck("qk_t1")
            t2, k2 = Fs()
            stt("dve", t2, ps[:, bs_, :], vec[:, gscol:gscol + 1], cs[:, 1, :], ALU.mult, ALU.mult, [PS(bs_), "vec", cskey], [k2])
            tt("dve", t1, t1, t2, ALU.add, [k1, k2], [k1])
            ck("qk_t3")
            tt("dve", dst, t1, rt, ALU.mult, [k1, rk], [dstkey])

        p1_early = set()

        def p1_n(tb):
            if tb % 2 == 0:
                return (lambda kc: (n_bf[:, kc, :], ("n", kc)))
            return (lambda kc: (U[:, kc, :], ("U", kc)))

        def p1_front(s, tb):
            h, hk = load_x(s, tb * 512)
            load_rope(cs1, "cs1", tb * 512)
            rmsnorm(h, hk, C_GMIX, p1_n(tb))

        def front_a(s, tb, pre=None):
            t0 = tb * 512
            h, hk = pre if pre is not None else load_x(s, t0)
            load_rope(cs2, "cs2", t0)
            rmsnorm(h, hk, C_GMIX, nbf_out)
            return dict(h=h, hk=hk, t0=t0, s=s, tb=tb)

        def front_b(ctx):
            wp, wk = next_piece(PI_BLK0 + 0)
            for c in range(4):
                qd, qk_ = Us(c)
                qk_chunk(wp, wk, c * 1024, (4 + c) * 1024, C_GQ, C_GQS, cs2, "cs2", qd, qk_)
            ck("q")

        def pool_a(tb):
            t0 = tb * 512
            for g in range(4):
                b = bank1()
                mm(ps[:, b, :], zero_bf[:], zero_rhs[:], True, False, ["zero"], [PS(b)])
                tts = [x for x in range(tb * 4 - 1, tb * 4 + 5) if 0 <= x < 16]
                for i, ttk in enumerate(tts):
                    n0 = 128 * ttk - 8 - t0
                    lo, hi_ = max(n0, 0), min(n0 + 144, 512)
                    var = 0 if ttk == 0 else (2 if ttk == 15 else 1)
                    mm(ps[:, b, lo:hi_], utok[:, ttk, g * 128:(g + 1) * 128], band[:, var * 4 + g, lo - n0:hi_ - n0],
                       False, i == len(tts) - 1, [("ut", ttk), "band"], [PS(b)])
                pd, pk = Us(8 + g)
                copy("act", pd, ps[:, b, :], [PS(b)], [pk])

        def pool_b():
            for g in range(4):
                pd, pk = Us(8 + g)
                b2 = bank1()
                mm(ps[:, b2, :], poolw[:, g, :], pd, True, True, ["poolw", pk], [PS(b2)])
                ts("dve", pmT[:, g, :], ps[:, b2, :], vec[:, C_PS + g:C_PS + g + 1], ALU.mult, [PS(b2), "vec"], [("pm", g)])

        def attention():
            items = [(c, kt) for c in range(4) for kt in range(16)]
            r_of = {}

            def qk(i):
                c, kt = items[i]
                kvh = c // 2
                qd, qk_ = Us(c)
                bS = 2 * (i % 2)
                for e_ in range(2):
                    mm(ps[:, bS + e_, :], kT2[e_ * 64:(e_ + 1) * 64, kvh, kt * 128:(kt + 1) * 128],
                       qd[e_ * 64:(e_ + 1) * 64, :], True, True, [("kT", kvh, kt // 4), qk_], [PS(bS + e_)])

            def ex(i):
                bS = 2 * (i % 2)
                r = ptrot.next()
                r_of[i] = r
                act(PT[:, r, :, :], ps[:, bS:bS + 2, :], AF.Exp, [PS(bS), PS(bS + 1)], [("PT", r)], scale=0.125)

            def pv(i):
                c, kt = items[i]
                kvh = c // 2
                r = r_of[i]
                bE, bO = (4, 5) if c % 2 == 0 else (6, 7)
                mm(ps[:, bE, :], Vpad[:, kt, kvh, 64:192], PT[:, r, 0, :], kt == 0, kt == 15,
                   [("V", kt), "Vpad", ("PT", r)], [PS(bE)])
                mm(ps[:, bO, :], Vpad[:, kt, kvh, 0:128], PT[:, r, 1, :], kt == 0, kt == 15,
                   [("V", kt), "Vpad", ("PT", r)], [PS(bO)])
                if kt == 15:
                    rc, rck = Fs()
                    recip(rc[64:128, :], ps[64:128, bE, :], [PS(bE)], [rck])
                    recip(rc[0:64, :], ps[0:64, bO, :], [PS(bO)], [rck])
                    od, ok_ = Us(4 + c)
                    tt("dve", od[0:64, :], ps[0:64, bE, :], rc[64:128, :], ALU.mult, [PS(bE), rck], [ok_])
                    tt("dve", od[64:128, :], ps[64:128, bO, :], rc[0:64, :], ALU.mult, [PS(bO), rck], [ok_])

            qk(0)
            for i in range(len(items)):
                ex(i)
                if i + 1 < len(items):
                    qk(i + 1)
                pv(i)
            bank_i[0] = 0

        def back(s, tb, ctx):
            h, hk, t0 = ctx["h"], ctx["hk"], ctx["t0"]
            dbg0 = debug and tb == 0 and s == 0
            if dbg0:
                dump("qT", U[:, 0, :], ("U", 0), 512)
            pre = load_x(s, (tb + 1) * 512) if tb < 3 else None
            if s == 0 and tb == 1 and n_seq > 1:
                for pid in (PI_W1, PI_W1B, PI_XKV0, PI_XKV1):
                    S.dma("pool", lambda e, pid=pid: e.dma_start(out=wsc[pid], in_=wst[pid]), ("cast", pid),
                          writes=[("wsc", pid)])
            attention()
            if dbg0:
                dump("oT", U[:, 4, :], ("U", 4), 512)
            ck("attn")
            if tb == 0:
                pool_a(tb)
                pool_b()
            if dbg0:
                dump("pm", pmT[:, 0, :], ("pm", 0), 512)
            ck("pool")
            for jp in range(4):
                wp, wk = next_piece(PI_BLK0 + 1 + jp)
                gslots = []
                for jj in range(2):
                    j = jp * 2 + jj
                    base = jj * 24 * 128
                    bga, bgp = bank1(), bank1()
                    for kc in range(8):
                        mm(ps[:, bga, :], wp[:, base + kc * 128: base + (kc + 1) * 128], n_bf[:, kc, :], kc == 0, kc == 7,
                           [wk, ("n", kc)], [PS(bga)])
                    for kc in range(8):
                        mm(ps[:, bgp, :], wp[:, base + (8 + kc) * 128: base + (9 + kc) * 128], n_bf[:, kc, :], kc == 0,
                           kc == 7, [wk, ("n", kc)], [PS(bgp)])
                    ea, eak = Fs()
                    act(ea, ps[:, bga, :], AF.Exp, [PS(bga), "nbg"], [eak], bias=nbg[:, j:j + 1], scale=-1.0)
                    ep, epk = Fs()
                    act(ep, ps[:, bgp, :], AF.Exp, [PS(bgp), "nbg"], [epk], bias=nbg[:, 8 + j:9 + j], scale=-1.0)
                    act(ea, ea, AF.Ln, [eak], [eak], bias=1.0, scale=1.0)
                    act(ep, ep, AF.Ln, [epk], [epk], bias=1.0, scale=1.0)
                    act(ea, ea, AF.Exp, [eak], [eak], scale=-1.0)
                    act(ep, ep, AF.Exp, [epk], [epk], scale=-1.0)
                    gslots.append((ea, eak, ep, epk))
                for jj in range(2):
                    j = jp * 2 + jj
                    base = jj * 24 * 128
                    ea, eak, ep, epk = gslots[jj]
                    ba, bp = bank1(), bank1()
                    for kc in range(4):
                        mm(ps[:, ba, :], wp[:, base + (16 + kc) * 128: base + (17 + kc) * 128], U[:, 4 + kc, :], kc == 0,
                           kc == 3, [wk, ("U", 4 + kc)], [PS(ba)])
                    for kc in range(4):
                        mm(ps[:, bp, :], wp[:, base + (20 + kc) * 128: base + (21 + kc) * 128], pmT[:, kc, :], kc == 0,
                           kc == 3, [wk, ("pm", kc)], [PS(bp)])
                    tt("dve", ea, ps[:, ba, :], ea, ALU.mult, [PS(ba), eak], [eak])
                    tt("dve", ep, ps[:, bp, :], ep, ALU.mult, [PS(bp), epk], [epk])
                    xd, xk_ = Us(16 + j)
                    tt("pool", xd, ea, ep, ALU.add, [eak, epk], [xk_])
            wp, wk = next_piece(PI_BLK0 + 5)
            for j in range(8):
                b = bank1()
                for kc in range(8):
                    mm(ps[:, b, :], wp[:, (j * 8 + kc) * 128:(j * 8 + kc + 1) * 128], U[:, 16 + kc, :], kc == 0, kc == 7,
                       [wk, ("U", 16 + kc)], [PS(b)])
                tt("dve", h[:, j, :], ps[:, b, :], h[:, j, :], ALU.add, [PS(b), hk[j]], [hk[j]])
            if dbg0:
                dump("h1", h[:, 0, :], hk[0], 512)
            ck("h1")
            scale_g(h, hk, C_GCROSS)
            r2t, r2k = rms_stats(h, hk)
            if tb < 3:
                pool_a(tb + 1)
            wp, wk = next_piece(PI_BLK0 + 6)
            for j in range(8):
                b = bank1()
                for kc in range(8):
                    mm(ps[:, b, :], wp[:, (j * 8 + kc) * 128:(j * 8 + kc + 1) * 128], n_bf[:, kc, :], kc == 0, kc == 7,
                       [wk, ("n", kc)], [PS(b)])
                xd, xk_ = Us(j)
                tt("dve", xd, ps[:, b, :], r2t, ALU.mult, [PS(b), r2k], [xk_])
            if tb < 3:
                pool_b()
            xs = {}

            def x_s(xh):
                bS = bank2()
                xs[xh] = bS
                for mt in range(2):
                    for dc in range(2):
                        mm(ps[:, bS + mt, :], xkT[:, 2 * xh + dc, mt * 128:(mt + 1) * 128], U[:, 2 * xh + dc, :],
                           dc == 0, dc == 1, [("xk", 2 * xh + dc), ("U", 2 * xh + dc)], [PS(bS + mt)])

            def x_e(xh):
                bS = xs[xh]
                r = ptrot.next()
                xs[("r", xh)] = r
                act(PT[:, r, :, :], ps[:, bS:bS + 2, :], AF.Exp, [PS(bS), PS(bS + 1)], [("PT", r)], scale=1.0 / 16.0)

            def x_pv(xh):
                r = xs[("r", xh)]
                bsum = bank1()
                for mt in range(2):
                    mm(ps[:, bsum, :], ones_bf[:], PT[:, r, mt, :], mt == 0, mt == 1, ["ones", ("PT", r)], [PS(bsum)])
                rc, rck = Fs()
                act(rc, ps[:, bsum, :], AF.Ln, [PS(bsum)], [rck])
                act(rc, rc, AF.Exp, [rck], [rck], scale=-1.0)
                for dc in range(2):
                    b = bank1()
                    for mt in range(2):
                        mm(ps[:, b, :], xv[:, mt, (2 * xh + dc) * 128:(2 * xh + dc + 1) * 128], PT[:, r, mt, :],
                           mt == 0, mt == 1, [("xv", mt), ("PT", r)], [PS(b)])
                    od, ok_ = Us(8 + 2 * xh + dc)
                    tt("dve", od, ps[:, b, :], rc, ALU.mult, [PS(b), rck], [ok_])

            x_s(0)
            for xh in range(4):
                x_e(xh)
                if xh + 1 < 4:
                    x_s(xh + 1)
                x_pv(xh)
            wp, wk = next_piece(PI_BLK0 + 7)
            for j in range(8):
                b = bank1()
                for kc in range(8):
                    mm(ps[:, b, :], wp[:, (j * 8 + kc) * 128:(j * 8 + kc + 1) * 128], U[:, 8 + kc, :], kc == 0, kc == 7,
                       [wk, ("U", 8 + kc)], [PS(b)])
                tt("dve", h[:, j, :], ps[:, b, :], h[:, j, :], ALU.add, [PS(b), hk[j]], [hk[j]])
            if dbg0:
                dump("h2", h[:, 0, :], hk[0], 512)
            ck("h2")
            scale_g(h, hk, C_GFFN)
            r3t, r3k = rms_stats(h, hk, dest=(rfin[:], "rfin"), expo=-1.0)
            nctx = None
            for p_ in range(2):
                for half in range(2):
                    wp, wk = next_piece(PI_BLK0 + 8 + p_ * 4 + half)
                    for i8 in range(8):
                        i = half * 8 + i8
                        b = bank1()
                        for kc in range(8):
                            mm(ps[:, b, :], wp[:, (i8 * 8 + kc) * 128:(i8 * 8 + kc + 1) * 128], n_bf[:, kc, :], kc == 0,
                               kc == 7, [wk, ("n", kc)], [PS(b)])
                        rl, rlk = Fs()
                        if i % 2 == 0:
                            act(rl, ps[:, b, :], AF.Relu, [PS(b)], [rlk])
                        else:
                            ts("dve", rl, ps[:, b, :], 0.0, ALU.max, [PS(b)], [rlk])
                        slot = (p_ * 16 + i) % 24
                        hd, hdk = Us(slot)
                        tt("pool" if i % 2 == 0 else "dve", hd, rl, rl, ALU.mult, [rlk], [hdk])
                if p_ == 1 and tb < 3:
                    nctx = front_a(s, tb + 1, pre)
                elif p_ == 1 and tb == 3 and s + 1 < n_seq:
                    p1_front(s + 1, 0)
                    p1_early.add(s + 1)
                for jh in range(2):
                    wp, wk = next_piece(PI_BLK0 + 10 + p_ * 4 + jh)
                    for j4 in range(4):
                        j = jh * 4 + j4
                        b = bank1()
                        for kc in range(16):
                            slot = (p_ * 16 + kc) % 24
                            mm(ps[:, b, :], wp[:, (j4 * 16 + kc) * 128:(j4 * 16 + kc + 1) * 128], U[:, slot, :], kc == 0,
                               kc == 15, [wk, ("U", slot)], [PS(b)])
                        ft, fk = Fs()
                        tt("dve", ft, ps[:, b, :], r3t, ALU.mult, [PS(b), r3k], [fk])
                        tt("dve", h[:, j, :], h[:, j, :], ft, ALU.add, [hk[j], fk], [hk[j]])
            ck("ffn")
            rr = rms_stats(h, hk, dest=(rfin[:], "rfin"))
            if nctx is not None:
                front_b(nctx)
            rms_apply(h, hk, C_GFINAL, lambda kc: (h[:, kc, :], hk[kc]), rr)
            ck("fnorm")
            dst = outT[s].rearrange("(kc p) t -> p kc t", p=128)[:, :, t0:t0 + 512]
            S.dma("sp", lambda e: e.dma_start(out=dst, in_=h[:]), ("o", hk[0][1]), reads=hk, writes=[("outT", s, tb)])
            out_keys.append(("outT", s, tb))
            ck("blk")
            return nctx

        try:
          for s in range(n_seq):
              wp, wk = next_piece(PI_W1, ahead=1)
              wpb, wkb = next_piece(PI_W1B, ahead=1)
              if s == 0:
                  S.op("pool", lambda e: e.memset(Vpad[:], 1.0), writes=["Vpad"])
              mnv = [U[:, 8 + kc // 2, (kc % 2) * 256:(kc % 2) * 256 + 256] for kc in range(8)]
              mnk = [("U", 8 + kc // 2) for kc in range(8)]
              memh = {}

              def k_stage(tb):
                  nfn = p1_n(tb)
                  for kvh in range(2):
                      qk_chunk(wp, wk, kvh * 2048, kvh * 2048 + 1024, C_GK, C_GKS, cs1, "cs1",
                               kT2[:, kvh, tb * 512:tb * 512 + 512], ("kT", kvh, tb), nfn)

              def v_stage(tb):
                  nfn = p1_n(tb)
                  for tt_ in range(4):
                      kt = tb * 4 + tt_
                      bv = bank1()
                      for kc in range(8):
                          na, nk = nfn(kc)
                          mm(ps[:, bv, 0:128], na[:, tt_ * 128:(tt_ + 1) * 128],
                             wp[:, 4096 + kc * 128: 4096 + (kc + 1) * 128], kc == 0, kc == 7, [wk, nk], [PS(bv)])
                      copy("act", Vpad[:, kt, :, 64:128], ps[:, bv, 0:128].rearrange("p (a d) -> p a d", d=64),
                           [PS(bv), "Vpad"], [("V", kt)])

              def u_stage(tb):
                  nfn = p1_n(tb)
                  for tt_ in range(4):
                      kt = tb * 4 + tt_
                      bu = bank1()
                      for kc in range(8):
                          na, nk = nfn(kc)
                          mm(ps[:, bu, :], na[:, tt_ * 128:(tt_ + 1) * 128], wpb[:, kc * 512:(kc + 1) * 512],
                             kc == 0, kc == 7, [wkb, nk], [PS(bu)])
                      copy("dve", utok[:, kt, :], ps[:, bu, :], [PS(bu)], [("ut", kt)])

              def mem_load():
                  hi = hcount[0] % 2
                  hcount[0] += 1
                  hm = hbuf[hi]
                  hmk = [("h", hi, kc) for kc in range(8)]
                  S.dma("sp", lambda e, hm=hm, s=s: e.dma_start(out=hm[:, :, 0:256],
                                                                  in_=memT[s].rearrange("(kc p) t -> p kc t", p=128)),
                        ("x", hi), writes=hmk)
                  memh["h"], memh["k"] = hm, hmk

              def mem_norm():
                  rmsnorm(memh["h"], memh["k"], C_GMEM, lambda kc: (mnv[kc], mnk[kc]), width=256)

              def xk_stage(wx, wxk):
                  for j in range(8):
                      b = bank1()
                      for kc in range(8):
                          mm(ps[:, b, 0:256], wx[:, (j * 8 + kc) * 128:(j * 8 + kc + 1) * 128], mnv[kc], kc == 0, kc == 7,
                             [wxk, mnk[kc]], [PS(b)])
                      copy("act" if j % 2 else "dve", xkT[:, j, :], ps[:, b, 0:256], [PS(b)], [("xk", j)])

              def xv_stage(wx, wxk):
                  for mt in range(2):
                      for jq in range(2):
                          b = bank1()
                          for jj in range(4):
                              j = jq * 4 + jj
                              for kc in range(8):
                                  mm(ps[:, b, jj * 128:(jj + 1) * 128], mnv[kc][:, mt * 128:(mt + 1) * 128],
                                     wx[:, (j * 8 + kc) * 128:(j * 8 + kc + 1) * 128], kc == 0, kc == 7, [wxk, mnk[kc]],
                                     [PS(b)])
                          copy("act" if jq % 2 else "dve", xv[:, mt, jq * 512:(jq + 1) * 512], ps[:, b, :], [PS(b)],
                               [("xv", mt)])

              if s not in p1_early:
                  p1_front(s, 0)
              for tb in range(3):
                  ck("p1_norm")
                  k_stage(tb)
                  ck("p1_k")
                  p1_front(s, tb + 1)
                  if tb == 2:
                      mem_load()
                  v_stage(tb)
                  u_stage(tb)
                  ck("p1_blk")
              k_stage(3)
              v_stage(3)
              wx0, wx0k = next_piece(PI_XKV0, ahead=1)
              mem_norm()
              u_stage(3)
              if debug:
                  dump("kT", kT2[:, 0, 0:512], ("kT", 0, 0), 512)
                  dump("V", Vpad[:, 0, 0, 64:128], ("V", 0), 64)
                  dump("ut", utok[:, 0, :], ("ut", 0), 512)
              ck("p1")
              xk_stage(wx0, wx0k)
              wx1, wx1k = next_piece(PI_XKV1)
              ctx = front_a(s, 0)
              xv_stage(wx1, wx1k)
              if debug:
                  dump("xk", xkT[:, 0, :], ("xk", 0), 256)
                  dump("xv", xv[:, 0, 0:512], ("xv", 0), 512)
              ck("mem")
              front_b(ctx)
              for tb in range(4):
                  ctx = back(s, tb, ctx)
        except _Stop:
            pass
        S.op("sp", lambda e: e.nop(), reads=out_keys + ["dbgd"])
        S.emit(nc, st)
    return nc


def _cm(w):
    K, N = w.shape
    return w.reshape(K // 128, 128, N // 128, 128).transpose(2, 1, 0, 3)


def _perm64():
    perm = np.zeros(64, np.int64)
    perm_sw = np.zeros(64, np.int64)
    for half in range(2):
        for axis in range(2):
            for f in range(16):
                ip = half * 32 + axis * 16 + f
                perm[ip] = axis * 32 + half * 16 + f
                perm_sw[ip] = axis * 32 + (1 - half) * 16 + f
    return perm, perm_sw


def _pack_weights(w_in, w_attn_up, w_pool_up, w_out, w_xq, w_xkv, w_xo, w_ff1, w_ff2):
    perm, perm_sw = _perm64()
    wst = np.zeros((NPIECE, 128, PW), np.float32)

    def flat(a):
        return np.ascontiguousarray(a).reshape(128, -1)

    def put(pid, off, a):
        a = flat(a)
        wst[pid, :, off:off + a.shape[1]] = a
        return off + a.shape[1]

    off = 0
    for kvh in range(2):
        cols = 512 + kvh * 64 + perm
        cols_sw = 512 + kvh * 64 + perm_sw
        for cc in (cols, cols_sw):
            wk = w_in[:, np.concatenate([cc, cc])]
            off = put(PI_W1, off, _cm(wk)[0])
    off = put(PI_W1, off, _cm(w_in[:, 640:768])[0])
    put(PI_W1B, 0, w_in[:, 768:1280].reshape(8, 128, 512).transpose(1, 0, 2))
    cx = _cm(w_xkv)
    put(PI_XKV0, 0, cx[0:8].transpose(1, 0, 2, 3))
    put(PI_XKV1, 0, cx[8:16].transpose(1, 0, 2, 3))
    qcols = np.concatenate([h * 64 + perm for h in range(8)])
    qscols = np.concatenate([h * 64 + perm_sw for h in range(8)])
    off = put(PI_BLK0 + 0, 0, _cm(w_in[:, qcols]).transpose(1, 0, 2, 3))
    put(PI_BLK0 + 0, off, _cm(w_in[:, qscols]).transpose(1, 0, 2, 3))
    cga = _cm(w_in[:, 1280:2304])
    cgp = _cm(w_in[:, 2304:3328])
    cau = _cm(w_attn_up)
    cpu = _cm(w_pool_up)
    for jp in range(4):
        off = 0
        for jj in range(2):
            j = jp * 2 + jj
            off = put(PI_BLK0 + 1 + jp, off, cga[j])
            off = put(PI_BLK0 + 1 + jp, off, cgp[j])
            off = put(PI_BLK0 + 1 + jp, off, cau[j])
            off = put(PI_BLK0 + 1 + jp, off, cpu[j])
    put(PI_BLK0 + 5, 0, _cm(w_out).transpose(1, 0, 2, 3))
    put(PI_BLK0 + 6, 0, _cm(w_xq).transpose(1, 0, 2, 3))
    put(PI_BLK0 + 7, 0, _cm(w_xo).transpose(1, 0, 2, 3))
    c1 = _cm(w_ff1)
    c2 = _cm(w_ff2)
    for p_ in range(2):
        for half in range(2):
            i0 = p_ * 16 + half * 8
            put(PI_BLK0 + 8 + p_ * 4 + half, 0, c1[i0:i0 + 8].transpose(1, 0, 2, 3))
        for jh in range(2):
            put(PI_BLK0 + 10 + p_ * 4 + jh, 0, c2[jh * 4:(jh + 1) * 4, :, p_ * 16:(p_ + 1) * 16, :].transpose(1, 0, 2, 3))
    return wst


def _const_tables():
    S_, GW = 2048, 64
    t = np.arange(S_)
    posn = np.stack([t // GW, t % GW], 0).astype(np.float32)
    inv = (10000.0 ** (-np.arange(0, 32, 2, dtype=np.float32) / 32.0)).astype(np.float32)
    rope = np.zeros((2, 128, S_), np.float32)
    for hh in range(2):
        for half in range(2):
            for axis in range(2):
                for f in range(16):
                    p = hh * 64 + half * 32 + axis * 16 + f
                    ang = (posn[axis] * inv[f]).astype(np.float32)
                    rope[0, p] = np.cos(ang)
                    rope[1, p] = np.sin(ang) * (-1.0 if half == 0 else 1.0)
    band = np.zeros((128, 3, 4, 144), np.float32)
    for var, ttk in enumerate((0, 5, 15)):
        for g, w in enumerate((2, 4, 8, 16)):
            for n in range(-8, 136):
                tp = 128 * ttk + n
                if tp < 0 or tp >= S_:
                    continue
                lo = min(max(tp - w // 2, 0), S_)
                hi = min(max(tp + (w - w // 2), 0), S_)
                cnt = float(hi - lo)
                for k in range(128):
                    tk = 128 * ttk + k
                    v = 0.0
                    if lo <= tk < hi:
                        v += 1.0 / cnt
                    if tk == tp:
                        v -= 1.0
                    band[k, var, g, n + 8] = v
    return rope, band.reshape(128, 12 * 144)


def _prep(inputs):
    g = lambda k: np.asarray(inputs[k], dtype=np.float32)
    x, mem = g("x"), g("mem")
    perm, perm_sw = _perm64()
    wst = _pack_weights(g("w_in")[0], g("w_attn_up")[0], g("w_pool_up")[0], g("w_out")[0], g("w_xq")[0],
                        g("w_xkv")[0], g("w_xo")[0], g("w_ff1")[0], g("w_ff2")[0])
    vec = np.zeros((128, 64), np.float32)
    for c0, name in ((C_GMIX, "g_mix"), (C_GCROSS, "g_cross"), (C_GFFN, "g_ffn"), (C_GMEM, "g_mem")):
        vec[:, c0:c0 + 8] = g(name)[0].reshape(8, 128).T
    vec[:, C_GFINAL:C_GFINAL + 8] = g("g_final").reshape(8, 128).T
    vec[:, C_BG:C_BG + 16] = g("b_gate")[0].reshape(16, 128).T
    vec[:, C_PS:C_PS + 4] = g("pool_scale")[0].reshape(4, 128).T
    gq, gk = g("g_q")[0], g("g_k")[0]
    vec[:, C_GQ] = np.tile(gq[perm], 2)
    vec[:, C_GQS] = np.tile(gq[perm_sw], 2)
    vec[:, C_GK] = np.tile(gk[perm], 2)
    vec[:, C_GKS] = np.tile(gk[perm_sw], 2)
    poolw = np.ascontiguousarray(g("pool_w")[0].transpose(1, 0, 2)).reshape(128, 512)
    rope, band = _const_tables()
    xT = np.ascontiguousarray(x.transpose(0, 2, 1))
    memT = np.ascontiguousarray(mem.transpose(0, 2, 1))
    in_maps = []
    for c in range(8):
        in_maps.append(dict(xT=xT[2 * c:2 * c + 2], memT=memT[2 * c:2 * c + 2], wst=wst, vec=vec, rope=rope,
                            band=band, poolw=poolw))
    return in_maps


def kernel(**inputs):
    in_maps = _prep(inputs)
    nc = build_core()
    res = run_bass_kernel_spmd(nc, in_maps, core_ids=list(range(8)))
    outT = np.stack([r["outT"] for r in res.results], 0)
    out = outT.reshape(16, 1024, 2048).transpose(0, 2, 1)
    return np.ascontiguousarray(out).astype(np.float32)
```
